# Optimizing a Trainium2 kernel written in Bass

```python
import jax, jax.numpy as jnp
from jax import lax
import numpy as np

D_MODEL = 4096
BATCH = 2
SEQ = 8192
DEPTH = 2

N_MIXERS = 2
N_A = (DEPTH + 1) // 2
N_B = DEPTH // 2
SHORT_CONV_K = 3
CONFORMER_K = 31
CONFORMER_EXP = 2
D_FF = 11008
FFN_CONV_K = 3
RMS_EPS = 1e-6
LN_EPS = 1e-5

kernel_name = "hybrid_shortconv_conformer_convffn"


def rmsnorm(x, g):
    xf = x.astype(jnp.float32)
    y = xf * lax.rsqrt(jnp.mean(xf * xf, axis=-1, keepdims=True) + RMS_EPS)
    return (y * g.astype(jnp.float32)).astype(x.dtype)


def layernorm(x, g, b):
    xf = x.astype(jnp.float32)
    mu = jnp.mean(xf, axis=-1, keepdims=True)
    var = jnp.mean(jnp.square(xf - mu), axis=-1, keepdims=True)
    y = (xf - mu) * lax.rsqrt(var + LN_EPS)
    return (y * g.astype(jnp.float32) + b.astype(jnp.float32)).astype(x.dtype)


def causal_dwconv(x, w):
    k, c = w.shape
    xp = jnp.pad(x, ((0, 0), (k - 1, 0), (0, 0)))
    return lax.conv_general_dilated(
        xp, w[:, None, :].astype(x.dtype), window_strides=(1,), padding="VALID",
        dimension_numbers=("NWC", "WIO", "NWC"), feature_group_count=c)


def short_conv_mixer(x, w_in, conv_w, w_out):
    gate_b, gate_c, h = jnp.split(jnp.einsum("bsd,de->bse", x, w_in), 3, axis=-1)
    h = causal_dwconv(gate_c * h, conv_w)
    return jnp.einsum("bsd,de->bse", gate_b * h, w_out)


def conformer_conv_mixer(x, w_pw1, b_pw1, dw_w, dw_b, ln_g, ln_b, w_pw2, b_pw2):
    h = jnp.einsum("bsd,de->bse", x, w_pw1) + b_pw1
    val, gate = jnp.split(h, 2, axis=-1)
    h = val * jax.nn.sigmoid(gate)
    h = causal_dwconv(h, dw_w) + dw_b
    h = layernorm(h, ln_g, ln_b)
    h = jax.nn.silu(h)
    return jnp.einsum("bsd,de->bse", h, w_pw2) + b_pw2


def conv_ffn(x, w_up, conv_w, w_down):
    h = causal_dwconv(jnp.einsum("bsd,df->bsf", x, w_up), conv_w)
    gate, val = jnp.split(h, 2, axis=-1)
    return jnp.einsum("bsf,fd->bsd", jax.nn.silu(gate) * val, w_down)


def setup_inputs(seed: int = 0) -> dict:
    key = jax.random.key(seed)
    ks = jax.random.split(key, 24)
    d, f = D_MODEL, D_FF
    nrm = lambda k, shape, fan_in: jax.random.normal(k, shape, jnp.float32) * (fan_in ** -0.5)
    gain = lambda k, shape: 1.0 + 0.05 * jax.random.normal(k, shape, jnp.float32)
    small = lambda k, shape: 0.01 * jax.random.normal(k, shape, jnp.float32)
    return {
        "x": jax.random.normal(ks[0], (BATCH, SEQ, d), jnp.float32),
        "g_mix_pre": gain(ks[1], (DEPTH, d)),
        "g_mix_post": gain(ks[2], (DEPTH, d)),
        "g_ffn_pre": gain(ks[3], (DEPTH, d)),
        "g_ffn_post": gain(ks[4], (DEPTH, d)),
        "a_w_in": nrm(ks[5], (N_A, d, 3 * d), d),
        "a_conv_w": nrm(ks[6], (N_A, SHORT_CONV_K, d), SHORT_CONV_K),
        "a_w_out": nrm(ks[7], (N_A, d, d), d),
        "b_w_pw1": nrm(ks[8], (N_B, d, CONFORMER_EXP * d), d),
        "b_b_pw1": small(ks[9], (N_B, CONFORMER_EXP * d)),
        "b_dw_w": nrm(ks[10], (N_B, CONFORMER_K, d), CONFORMER_K),
        "b_dw_b": small(ks[11], (N_B, d)),
        "b_ln_g": gain(ks[12], (N_B, d)),
        "b_ln_b": small(ks[13], (N_B, d)),
        "b_w_pw2": nrm(ks[14], (N_B, d, d), d),
        "b_b_pw2": small(ks[15], (N_B, d)),
        "f_w_up": nrm(ks[16], (DEPTH, d, 2 * f), d),
        "f_conv_w": nrm(ks[17], (DEPTH, FFN_CONV_K, 2 * f), FFN_CONV_K),
        "f_w_down": nrm(ks[18], (DEPTH, f, d), f),
    }


def reference(x, g_mix_pre, g_mix_post, g_ffn_pre, g_ffn_post,
              a_w_in, a_conv_w, a_w_out,
              b_w_pw1, b_b_pw1, b_dw_w, b_dw_b, b_ln_g, b_ln_b, b_w_pw2, b_b_pw2,
              f_w_up, f_conv_w, f_w_down):
    for i in range(DEPTH):
        h = rmsnorm(x, g_mix_pre[i])
        j = i // N_MIXERS
        if i % N_MIXERS == 0:
            h = short_conv_mixer(h, a_w_in[j], a_conv_w[j], a_w_out[j])
        else:
            h = conformer_conv_mixer(h, b_w_pw1[j], b_b_pw1[j], b_dw_w[j], b_dw_b[j],
                                     b_ln_g[j], b_ln_b[j], b_w_pw2[j], b_b_pw2[j])
        x = x + rmsnorm(h, g_mix_post[i])
        h = conv_ffn(rmsnorm(x, g_ffn_pre[i]), f_w_up[i], f_conv_w[i], f_w_down[i])
        x = x + rmsnorm(h, g_ffn_post[i])
    return x
```

```python
import numpy as np
from contextlib import ExitStack

import concourse.bass as bass
import concourse.mybir as mybir
from concourse.bass_utils import run_bass_kernel_spmd

F32 = mybir.dt.float32
BF16 = mybir.dt.bfloat16
AF = mybir.ActivationFunctionType
ALU = mybir.AluOpType

D = 4096
KC = 32
DFF = 11008
FC = 86
HKC = 43
SEQ = 8192
BATCH = 2
NCORE = 8
CHUNK = 2048
RMS_EPS = 1e-6
LN_EPS = 1e-5
CK = 31

SAME_ENGINE_SYNC = False

OFF = {}
_o = 0
for _name, _n in [("g_mix_pre0", 32), ("g_mix_pre1", 32), ("g_mix_post0", 32), ("g_mix_post1", 32),
                  ("g_ffn_pre0", 32), ("g_ffn_pre1", 32), ("g_ffn_post0", 32), ("g_ffn_post1", 32),
                  ("a_conv", 96), ("b_b_pw1", 64), ("b_dw_w", 32 * CK), ("b_dw_b", 32), ("b_ln_g", 32),
                  ("b_ln_b", 32), ("b_b_pw2", 32), ("f_conv0", 172 * 3), ("f_conv1", 172 * 3), ("mask", 1), ("ident", 128), ("ones", 128)]:
    OFF[_name] = _o
    _o += _n
NCONST = _o


def _merge(dst, src):
    for k, v in src.items():
        if dst.get(k, 0) < v:
            dst[k] = v


class Res:
    __slots__ = ("name", "lw", "rd")

    def __init__(self, name):
        self.name = name
        self.lw = {}
        self.rd = {}


class Region:
    def __init__(self, roles):
        self.roles = roles
        self.cur = None

    def use(self, role):
        if self.cur == role:
            return
        if self.cur is not None:
            lw, rd = {}, {}
            for r in self.roles[self.cur]:
                _merge(lw, r.lw)
                _merge(rd, r.rd)
            for r in self.roles[role]:
                r.lw = dict(lw)
                r.rd = dict(rd)
        self.cur = role


class Sched:
    def __init__(self, nc, es):
        self.nc = nc
        self.es = es
        self.engobj = {"pe": nc.tensor, "act": nc.scalar, "dve": nc.vector, "pool": nc.gpsimd, "sp": nc.sync}
        self.semh = {}
        self.cnt = {}
        self.waited = {e: {} for e in self.engobj}
        self.own = {}
        for e in ("pe", "act", "dve", "pool"):
            self.own[e] = self.newsem("e_" + e)

    def newsem(self, name):
        h = self.es.enter_context(self.nc.semaphore(name))
        self.semh[name] = h
        self.cnt[name] = 0
        return name

    def _deps(self, e, reads, writes):
        need = {}
        for r in reads:
            _merge(need, r.lw)
        for w in writes:
            _merge(need, w.lw)
            _merge(need, w.rd)
        own = self.own.get(e)
        wd = self.waited[e]
        eng = self.engobj[e]
        for k, v in need.items():
            if k == own and not SAME_ENGINE_SYNC:
                continue
            if wd.get(k, 0) >= v:
                continue
            eng.wait_ge(self.semh[k], v)
            wd[k] = v

    def _commit(self, k, v, reads, writes):
        for r in reads:
            if r.rd.get(k, 0) < v:
                r.rd[k] = v
        for w in writes:
            w.lw = {k: v}
            w.rd = {}

    def op(self, e, reads, writes, fn):
        self._deps(e, reads, writes)
        ins = fn(self.engobj[e])
        k = self.own[e]
        self.cnt[k] += 1
        ins.then_inc(self.semh[k], 1)
        self._commit(k, self.cnt[k], reads, writes)

    def group(self, e, reads, writes, fns):
        self._deps(e, reads, writes)
        ins = None
        eng = self.engobj[e]
        for fn in fns:
            ins = fn(eng)
        k = self.own[e]
        self.cnt[k] += 1
        ins.then_inc(self.semh[k], 1)
        self._commit(k, self.cnt[k], reads, writes)

    def dma(self, q, semk, out_ap, in_ap, reads, writes):
        self._deps(q, reads, writes)
        ins = self.engobj[q].dma_start(out=out_ap, in_=in_ap)
        self.cnt[semk] += 16
        ins.then_inc(self.semh[semk], 16)
        self._commit(semk, self.cnt[semk], reads, writes)

    def final_wait(self, e, res_list):
        need = {}
        for r in res_list:
            _merge(need, r.lw)
            _merge(need, r.rd)
        for k, v in need.items():
            self.engobj[e].wait_ge(self.semh[k], v)


def build_program(T, NTILE, NRUN, HALO, layers=(0, 1), nslot=3):
    NT = T * NTILE
    NOUT = NRUN * T - HALO
    NB = (T + 127) // 128
    nc = bass.Bass("TRN2", target_bir_lowering=False)
    x_d = nc.dram_tensor("xs", [NT, D], F32, kind="ExternalInput").ap()
    cs_d = nc.dram_tensor("consts", [128, NCONST], F32, kind="ExternalInput").ap()
    wd = {}
    for name, shp in [("a_w_in", [D, 3 * D]), ("a_w_out", [D, D]), ("b_w_pw1", [D, 2 * D]), ("b_w_pw2", [D, D]),
                      ("f_w_up0", [D, 2 * DFF]), ("f_w_up1", [D, 2 * DFF]),
                      ("f_w_down0", [DFF, D]), ("f_w_down1", [DFF, D])]:
        wd[name] = nc.dram_tensor(name, shp, F32, kind="ExternalInput").ap().rearrange("(kc p) n -> p kc n", p=128)
    y_d = nc.dram_tensor("y", [NOUT, D], F32, kind="ExternalOutput").ap()

    es = ExitStack()
    with es:
        S = Sched(nc, es)
        sb = lambda name, shape, dt: es.enter_context(nc.sbuf_tensor(name, shape, dt))
        cs = sb("cs", [128, NCONST], F32)
        xres = sb("xres", [128, KC, T], F32)
        bx = sb("bx", [128, KC * T], F32)
        hidr = sb("hidr", [128, FC * T], BF16)
        slots = [sb(f"wslot{i}", [128, HKC, 128], BF16) for i in range(nslot)]
        EW = T + CK - 1
        ext = [sb(f"ext{i}", [128, EW], F32) for i in range(4)]
        acc = [sb(f"acc{i}", [128, T], F32) for i in range(4)]
        tmp = [sb(f"tmp{i}", [128, T], F32) for i in range(4)]
        sqb = [sb(f"sq{i}", [128, T], F32) for i in range(2)]
        bc = [sb(f"bc{i}", [128, T], F32) for i in range(3)]
        carA = sb("carA", [128, KC, 2], F32)
        carB = sb("carB", [128, KC, CK - 1], F32)
        carF = [sb(f"carF{l}", [128, 2 * FC, 2], F32) for l in range(2)]
        banks = [es.enter_context(nc.psum_tensor(f"bank{i}", [128, 512], F32)) for i in range(8)]

        ident = cs[:, OFF["ident"]:OFF["ident"] + 128]
        ones = cs[:, OFF["ones"]:OFF["ones"] + 128]
        bxb = bx[:].bitcast(BF16)
        hidf = hidr[:].bitcast(F32)
        xn = lambda k: bxb[:, k * T:(k + 1) * T]
        v1 = lambda m: bx[:, m * T:(m + 1) * T]
        hid = lambda f: hidr[:, f * T:(f + 1) * T]
        v2 = lambda m: hidf[:, m * T:(m + 1) * T]
        stg = lambda b: hidf[:, b * D:(b + 1) * D]
        assert NB * D <= FC * T // 2 and KC * T <= FC * T // 2

        R_cs = Res("cs")
        R_x = [Res(f"x{k}") for k in range(KC)]
        R_xn = [Res(f"xn{k}") for k in range(KC)]
        R_v1 = [Res(f"v1{k}") for k in range(KC)]
        R_hid = [Res(f"hid{k}") for k in range(FC)]
        R_v2 = [Res(f"v2{k}") for k in range(KC)]
        R_stg = [Res(f"stg{k}") for k in range(NB)]
        REG_bx = Region({"xn": R_xn, "v1": R_v1})
        REG_hid = Region({"hid": R_hid, "v2": R_v2, "stg": R_stg})
        R_slot = [Res(f"slot{i}") for i in range(nslot)]
        R_ext = [Res(f"ext{i}") for i in range(4)]
        R_acc = [Res(f"acc{i}") for i in range(4)]
        R_tmp = [Res(f"tmp{i}") for i in range(4)]
        R_sq = [Res(f"sq{i}") for i in range(2)]
        R_bc = [Res(f"bc{i}") for i in range(3)]
        R_carA = [Res(f"carA{k}") for k in range(KC)]
        R_carB = [Res(f"carB{k}") for k in range(KC)]
        R_carF = [[Res(f"carF{l}_{k}") for k in range(2 * FC)] for l in range(2)]
        R_const = R_cs
        R_bank = [Res(f"bank{i}") for i in range(8)]
        sem_slot = [S.newsem(f"d_slot{i}") for i in range(nslot)]
        sem_stg = [S.newsem(f"d_stg{i}") for i in range(NB)]
        sem_cs = S.newsem("d_cs")

        S.dma("sp", sem_cs, cs[:], cs_d[:], [], [R_cs])
        S.op("dve", [], R_carA, lambda e: e.memset(carA[:], 0.0))
        S.op("dve", [], R_carB, lambda e: e.memset(carB[:], 0.0))
        for l in range(2):
            S.op("dve", [], R_carF[l], lambda e, l=l: e.memset(carF[l][:], 0.0))

        def C(name, j=0, n=1):
            o = OFF[name] + j
            return cs[:, o:o + n]

        wstate = {"n": 0, "tile": 0, "s": 0}
        NSLAB = 2 * (96 + 32 + 172 + 64) if len(layers) == 2 else 96 + 32 + 172 + 64 + 64
        use_cache = NRUN > 1
        if use_cache:
            WCH = 160
            wcs = [nc.dram_tensor(f"wcache{j}", [min(WCH, NSLAB - j * WCH), 128, HKC * 128], BF16,
                                  kind="Internal").ap() for j in range((NSLAB + WCH - 1) // WCH)]
            wcache = [wcs[i // WCH][i % WCH] for i in range(NSLAB)]
            R_wc = [Res(f"wc{i}") for i in range(NSLAB)]
            sem_st = [S.newsem(f"d_st{i}") for i in range(nslot)]

        def load_slab(wname, kc0, nk, col0):
            i = wstate["n"] % nslot
            wstate["n"] += 1
            sidx = wstate["s"]
            wstate["s"] += 1
            if wstate["tile"] == 0 or not use_cache:
                S.dma("pool", sem_slot[i], slots[i][:, 0:nk, :], wd[wname][:, kc0:kc0 + nk, col0:col0 + 128],
                      [], [R_slot[i]])
                if use_cache:
                    S.dma("sp", sem_st[i], wcache[sidx][:, 0:nk * 128],
                          slots[i][:, 0:nk, :].rearrange("p k n -> p (k n)"), [R_slot[i]], [R_wc[sidx]])
            else:
                S.dma("sp", sem_slot[i], slots[i][:, 0:nk, :].rearrange("p k n -> p (k n)"),
                      wcache[sidx][:, 0:nk * 128], [R_wc[sidx]], [R_slot[i]])
            return i

        bank_rr = {"n": 0}

        def next_bank():
            b = bank_rr["n"] % 6
            bank_rr["n"] += 1
            return b

        def mm_group(bank, slab_list, rhs_fn, rhs_res):
            fns = []
            ktot = sum(nk for _, nk in slab_list)
            kk = 0
            for (si, nk) in slab_list:
                for j in range(nk):
                    fns.append(lambda e, si=si, j=j, kk=kk: e.matmul(
                        banks[bank][:, 0:T], lhsT=slots[si][:, j, :], rhs=rhs_fn(kk),
                        start=(kk == 0), stop=(kk == ktot - 1)))
                    kk += 1
            S.group("pe", [R_slot[si] for si, _ in slab_list] + rhs_res, [R_bank[bank]], fns)

        sq_rr = {"n": 0}

        def stats_acc(sbank, src_ap, src_res, first, last, func=AF.Square, bias=None, via_act=True):
            if via_act:
                i = sq_rr["n"] % 2
                sq_rr["n"] += 1
                if bias is None:
                    S.op("act", src_res, [R_sq[i]], lambda e: e.activation(out=sqb[i][:], in_=src_ap, func=func))
                else:
                    S.op("act", src_res + [R_cs], [R_sq[i]],
                         lambda e: e.activation(out=sqb[i][:], in_=src_ap, func=func, bias=bias))
                rhs, rres = sqb[i][:], [R_sq[i]]
            else:
                rhs, rres = src_ap, src_res
            S.op("pe", rres + [R_const], [R_bank[sbank]],
                 lambda e: e.matmul(banks[sbank][:, 0:T], lhsT=ones, rhs=rhs, start=first, stop=last))

        eps_t = sb("eps_t", [128, 2], F32)
        R_eps = Res("eps")
        S.op("dve", [], [R_eps], lambda e: e.memset(eps_t[:, 0:1], RMS_EPS))
        S.op("dve", [], [R_eps], lambda e: e.memset(eps_t[:, 1:2], LN_EPS))

        def prenorm(gname):
            REG_bx.use("xn")
            for k in range(KC):
                stats_acc(6, xres[:, k, :], [R_x[k]], k == 0, k == KC - 1)
            S.op("act", [R_bank[6], R_eps], [R_bc[0]],
                 lambda e: e.activation(out=bc[0][:], in_=banks[6][:, 0:T], func=AF.Sqrt,
                                        bias=eps_t[:, 0:1], scale=1.0 / D))
            S.op("dve", [R_bc[0]], [R_bc[0]], lambda e: e.reciprocal(bc[0][:], bc[0][:]))
            for k in range(KC):
                S.op("dve", [R_x[k], R_bc[0], R_cs], [R_xn[k]],
                     lambda e, k=k: e.scalar_tensor_tensor(out=xn(k), in0=xres[:, k, :], scalar=C(gname, k),
                                                           in1=bc[0][:], op0=ALU.mult, op1=ALU.mult))

        def postnorm(gname, o_fn, o_res):
            S.op("act", [R_bank[7], R_eps], [R_bc[1]],
                 lambda e: e.activation(out=bc[1][:], in_=banks[7][:, 0:T], func=AF.Sqrt,
                                        bias=eps_t[:, 0:1], scale=1.0 / D))
            S.op("dve", [R_bc[1]], [R_bc[1]], lambda e: e.reciprocal(bc[1][:], bc[1][:]))
            for m in range(KC):
                i = m % 4
                S.op("dve", [o_res[m], R_bc[1], R_cs], [R_tmp[i]],
                     lambda e, m=m, i=i: e.scalar_tensor_tensor(out=tmp[i][:], in0=o_fn(m), scalar=C(gname, m),
                                                                in1=bc[1][:], op0=ALU.mult, op1=ALU.mult))
                S.op("dve", [R_tmp[i], R_x[m]], [R_x[m]],
                     lambda e, m=m, i=i: e.tensor_tensor(out=xres[:, m, :], in0=xres[:, m, :], in1=tmp[i][:],
                                                         op=ALU.add))

        def out_proj(wname, nkc, rhs_fn, rhs_res, o_fn, o_res, bias_name=None):
            pieces = [(0, nkc)] if nkc <= HKC else [(0, HKC), (HKC, nkc - HKC)]
            for m in range(KC):
                sl = [(load_slab(wname, k0, nk, m * 128), nk) for (k0, nk) in pieces]
                b = next_bank()
                mm_group(b, sl, rhs_fn, rhs_res)
                if bias_name is None:
                    S.op("act", [R_bank[b]], [o_res[m]],
                         lambda e, m=m, b=b: e.activation(out=o_fn(m), in_=banks[b][:, 0:T], func=AF.Copy))
                    stats_acc(7, banks[b][:, 0:T], [R_bank[b]], m == 0, m == KC - 1)
                else:
                    S.op("act", [R_bank[b], R_cs], [o_res[m]],
                         lambda e, m=m, b=b: e.activation(out=o_fn(m), in_=banks[b][:, 0:T], func=AF.Identity,
                                                          bias=C(bias_name, m)))
                    stats_acc(7, banks[b][:, 0:T], [R_bank[b]], m == 0, m == KC - 1, bias=C(bias_name, m))

        ext_rr = {"n": 0}

        def conv3(src_bank, wname, widx, car, car_res, mask_halo):
            i = ext_rr["n"] % 4
            ext_rr["n"] += 1
            S.group("act", [car_res, R_bank[src_bank]], [R_ext[i]], [
                lambda e: e.activation(out=ext[i][:, 0:2], in_=car, func=AF.Copy),
                lambda e: e.activation(out=ext[i][:, 2:2 + T], in_=banks[src_bank][:, 0:T], func=AF.Copy)])
            if mask_halo:
                S.op("dve", [R_ext[i], R_cs], [R_ext[i]],
                     lambda e: e.tensor_scalar(out=ext[i][:, 2:2 + HALO], in0=ext[i][:, 2:2 + HALO],
                                               scalar1=C("mask"), scalar2=None, op0=ALU.mult))
            S.group("act", [R_ext[i], R_cs], [car_res, R_acc[i]], [
                lambda e: e.activation(out=car, in_=ext[i][:, T:T + 2], func=AF.Copy),
                lambda e: e.activation(out=acc[i][:], in_=ext[i][:, 0:T], func=AF.Copy,
                                       scale=C(wname, widx * 3 + 0))])
            for k in (1, 2):
                S.op("dve", [R_ext[i], R_acc[i], R_cs], [R_acc[i]],
                     lambda e, k=k: e.scalar_tensor_tensor(out=acc[i][:], in0=ext[i][:, k:k + T],
                                                           scalar=C(wname, widx * 3 + k), in1=acc[i][:],
                                                           op0=ALU.mult, op1=ALU.add))
            return i

        def ffn(l, first_tile):
            prenorm(f"g_ffn_pre{l}")
            REG_hid.use("hid")
            wname = f"f_w_up{l}"
            cname = f"f_conv{l}"
            mask = first_tile and l == 1
            for f in range(FC):
                sg = load_slab(wname, 0, KC, f * 128)
                sv = load_slab(wname, 0, KC, DFF + f * 128)
                bg, bv = next_bank(), next_bank()
                mm_group(bg, [(sg, KC)], xn, R_xn)
                mm_group(bv, [(sv, KC)], xn, R_xn)
                ig = conv3(bg, cname, f, carF[l][:, f, :], R_carF[l][f], mask)
                iv = conv3(bv, cname, FC + f, carF[l][:, FC + f, :], R_carF[l][FC + f], mask)
                S.op("act", [R_acc[ig]], [R_tmp[ig]],
                     lambda e: e.activation(out=tmp[ig][:], in_=acc[ig][:], func=AF.Silu))
                S.op("dve", [R_tmp[ig], R_acc[iv]], [R_hid[f]],
                     lambda e, f=f: e.tensor_tensor(out=hid(f), in0=tmp[ig][:], in1=acc[iv][:], op=ALU.mult))
            REG_bx.use("v1")
            out_proj(f"f_w_down{l}", FC, hid, R_hid, v1, R_v1)
            postnorm(f"g_ffn_post{l}", v1, R_v1)

        def mixer_a():
            prenorm("g_mix_pre0")
            REG_hid.use("hid")
            for m in range(KC):
                s_b = load_slab("a_w_in", 0, KC, m * 128)
                s_c = load_slab("a_w_in", 0, KC, D + m * 128)
                s_h = load_slab("a_w_in", 0, KC, 2 * D + m * 128)
                bb, bcn, bh = next_bank(), next_bank(), next_bank()
                mm_group(bb, [(s_b, KC)], xn, R_xn)
                mm_group(bcn, [(s_c, KC)], xn, R_xn)
                mm_group(bh, [(s_h, KC)], xn, R_xn)
                i = ext_rr["n"] % 4
                ext_rr["n"] += 1
                S.op("act", [R_bank[bcn]], [R_tmp[i]],
                     lambda e: e.activation(out=tmp[i][:], in_=banks[bcn][:, 0:T], func=AF.Copy))
                S.op("act", [R_carA[m]], [R_ext[i]],
                     lambda e: e.activation(out=ext[i][:, 0:2], in_=carA[:, m, :], func=AF.Copy))
                S.op("dve", [R_bank[bh], R_tmp[i]], [R_ext[i]],
                     lambda e: e.tensor_tensor(out=ext[i][:, 2:2 + T], in0=banks[bh][:, 0:T], in1=tmp[i][:],
                                               op=ALU.mult))
                S.group("act", [R_ext[i], R_cs], [R_carA[m], R_acc[i]], [
                    lambda e: e.activation(out=carA[:, m, :], in_=ext[i][:, T:T + 2], func=AF.Copy),
                    lambda e: e.activation(out=acc[i][:], in_=ext[i][:, 0:T], func=AF.Copy,
                                           scale=C("a_conv", m * 3 + 0))])
                for k in (1, 2):
                    S.op("dve", [R_ext[i], R_acc[i], R_cs], [R_acc[i]],
                         lambda e, k=k: e.scalar_tensor_tensor(out=acc[i][:], in0=ext[i][:, k:k + T],
                                                               scalar=C("a_conv", m * 3 + k), in1=acc[i][:],
                                                               op0=ALU.mult, op1=ALU.add))
                S.op("dve", [R_bank[bb], R_acc[i]], [R_hid[m]],
                     lambda e: e.tensor_tensor(out=hid(m), in0=banks[bb][:, 0:T], in1=acc[i][:], op=ALU.mult))
            REG_bx.use("v1")
            out_proj("a_w_out", KC, hid, R_hid[0:KC], v1, R_v1)
            postnorm("g_mix_post0", v1, R_v1)

        def mixer_b(first_tile):
            prenorm("g_mix_pre1")
            REG_hid.use("v2")
            NK = CK - 1
            for m in range(KC):
                s_v = load_slab("b_w_pw1", 0, KC, m * 128)
                s_g = load_slab("b_w_pw1", 0, KC, D + m * 128)
                bv, bg = next_bank(), next_bank()
                mm_group(bv, [(s_v, KC)], xn, R_xn)
                mm_group(bg, [(s_g, KC)], xn, R_xn)
                i = ext_rr["n"] % 4
                ext_rr["n"] += 1
                S.op("act", [R_bank[bg], R_cs], [R_tmp[i]],
                     lambda e: e.activation(out=tmp[i][:], in_=banks[bg][:, 0:T], func=AF.Sigmoid,
                                            bias=C("b_b_pw1", KC + m)))
                S.op("act", [R_carB[m]], [R_ext[i]],
                     lambda e: e.activation(out=ext[i][:, 0:NK], in_=carB[:, m, :], func=AF.Copy))
                S.op("dve", [R_bank[bv], R_tmp[i], R_cs], [R_ext[i]],
                     lambda e: e.scalar_tensor_tensor(out=ext[i][:, NK:NK + T], in0=banks[bv][:, 0:T],
                                                      scalar=C("b_b_pw1", m), in1=tmp[i][:],
                                                      op0=ALU.add, op1=ALU.mult))
                if first_tile:
                    S.op("dve", [R_ext[i], R_cs], [R_ext[i]],
                         lambda e: e.tensor_scalar(out=ext[i][:, NK:NK + HALO], in0=ext[i][:, NK:NK + HALO],
                                                   scalar1=C("mask"), scalar2=None, op0=ALU.mult))
                S.group("act", [R_ext[i], R_cs], [R_carB[m], R_acc[i]], [
                    lambda e: e.activation(out=carB[:, m, :], in_=ext[i][:, T:T + NK], func=AF.Copy),
                    lambda e: e.activation(out=acc[i][:], in_=ext[i][:, 0:T], func=AF.Identity,
                                           scale=C("b_dw_w", m * CK + 0), bias=C("b_dw_b", m))])
                for k in range(1, CK):
                    last = (k == CK - 1)
                    S.op("dve", [R_ext[i], R_acc[i], R_cs], [R_v2[m]] if last else [R_acc[i]],
                         lambda e, k=k, last=last: e.scalar_tensor_tensor(
                             out=(v2(m) if last else acc[i][:]), in0=ext[i][:, k:k + T],
                             scalar=C("b_dw_w", m * CK + k), in1=acc[i][:], op0=ALU.mult, op1=ALU.add))
                stats_acc(6, v2(m), [R_v2[m]], m == 0, m == KC - 1, via_act=False)
                stats_acc(7, v2(m), [R_v2[m]], m == 0, m == KC - 1)
            S.op("act", [R_bank[6]], [R_bc[0]],
                 lambda e: e.activation(out=bc[0][:], in_=banks[6][:, 0:T], func=AF.Copy, scale=1.0 / D))
            S.op("dve", [R_bc[0]], [R_bc[2]],
                 lambda e: e.tensor_tensor(out=bc[2][:], in0=bc[0][:], in1=bc[0][:], op=ALU.mult))
            S.op("dve", [R_bank[7], R_bc[2]], [R_bc[2]],
                 lambda e: e.scalar_tensor_tensor(out=bc[2][:], in0=banks[7][:, 0:T], scalar=1.0 / D, in1=bc[2][:],
                                                  op0=ALU.mult, op1=ALU.subtract))
            S.op("act", [R_bc[2], R_eps], [R_bc[2]],
                 lambda e: e.activation(out=bc[2][:], in_=bc[2][:], func=AF.Sqrt, bias=eps_t[:, 1:2], scale=1.0))
            S.op("dve", [R_bc[2]], [R_bc[2]], lambda e: e.reciprocal(bc[2][:], bc[2][:]))
            for m in range(KC):
                i = m % 4
                S.op("dve", [R_v2[m], R_bc[0]], [R_tmp[i]],
                     lambda e: e.tensor_tensor(out=tmp[i][:], in0=v2(m), in1=bc[0][:], op=ALU.subtract))
                S.op("dve", [R_tmp[i], R_bc[2]], [R_tmp[i]],
                     lambda e: e.tensor_tensor(out=tmp[i][:], in0=tmp[i][:], in1=bc[2][:], op=ALU.mult))
                S.op("act", [R_tmp[i], R_cs], [R_xn[m]],
                     lambda e: e.activation(out=xn(m), in_=tmp[i][:], func=AF.Silu,
                                            scale=C("b_ln_g", m), bias=C("b_ln_b", m)))
            out_proj("b_w_pw2", KC, xn, R_xn, v2, R_v2, bias_name="b_b_pw2")
            postnorm("g_mix_post1", v2, R_v2)

        for it in range(NRUN):
            t0 = it * T
            wstate["tile"] = it
            wstate["s"] = 0
            REG_hid.use("stg")
            for b in range(NB):
                n = min(128, T - b * 128)
                S.dma("pool", sem_stg[b], stg(b)[0:n, :], x_d[t0 + b * 128:t0 + b * 128 + n, :], [], [R_stg[b]])
            for c in range(KC):
                bk = next_bank()
                fns = []
                for b in range(NB):
                    n = min(128, T - b * 128)
                    fns.append(lambda e, b=b, n=n: e.transpose(
                        out=banks[bk][:, b * 128:b * 128 + n], in_=stg(b)[0:n, c * 128:(c + 1) * 128],
                        identity=ident[0:n, 0:n]))
                S.group("pe", R_stg + [R_const], [R_bank[bk]], fns)
                eng = "act" if c % 2 == 0 else "dve"
                if eng == "act":
                    S.op("act", [R_bank[bk]], [R_x[c]],
                         lambda e: e.activation(out=xres[:, c, :], in_=banks[bk][:, 0:T], func=AF.Copy))
                else:
                    S.op("dve", [R_bank[bk]], [R_x[c]],
                         lambda e: e.tensor_copy(out=xres[:, c, :], in_=banks[bk][:, 0:T]))
            for l in layers:
                if l == 0:
                    mixer_a()
                else:
                    mixer_b(it == 0)
                ffn(l, it == 0)
            REG_hid.use("stg")
            for b in range(NB):
                n = min(128, T - b * 128)
                lo = max(0, HALO - (t0 + b * 128))
                if lo >= n:
                    continue
                for c4 in range(KC // 4):
                    bk = next_bank()
                    fns = []
                    for j in range(4):
                        c = c4 * 4 + j
                        fns.append(lambda e, c=c, j=j: e.transpose(
                            out=banks[bk][0:n, j * 128:(j + 1) * 128], in_=xres[:, c, b * 128:b * 128 + n],
                            identity=ident))
                    S.group("pe", R_x[c4 * 4:c4 * 4 + 4] + [R_const], [R_bank[bk]], fns)
                    if c4 % 2 == 0:
                        S.op("act", [R_bank[bk]], [R_stg[b]],
                             lambda e: e.activation(out=stg(b)[0:n, c4 * 512:(c4 + 1) * 512], in_=banks[bk][0:n, :],
                                                    func=AF.Copy))
                    else:
                        S.op("dve", [R_bank[bk]], [R_stg[b]],
                             lambda e: e.tensor_copy(out=stg(b)[0:n, c4 * 512:(c4 + 1) * 512], in_=banks[bk][0:n, :]))
                g0 = t0 + b * 128 + lo - HALO
                S.dma("pool", sem_stg[b], y_d[g0:g0 + (n - lo), :], stg(b)[lo:n, :], [R_stg[b]], [])
        S.final_wait("pool", R_stg)
    return nc


def _fm(v):
    v = np.asarray(v, np.float32)
    return np.ascontiguousarray(v.reshape(-1, 128).T)


def pack_consts(inp, mask_val):
    cs = np.zeros((128, NCONST), np.float32)

    def put(name, arr):
        arr = np.asarray(arr, np.float32)
        cs[:, OFF[name]:OFF[name] + arr.shape[1]] = arr

    for l in range(2):
        put(f"g_mix_pre{l}", _fm(inp["g_mix_pre"][l]))
        put(f"g_mix_post{l}", _fm(inp["g_mix_post"][l]))
        put(f"g_ffn_pre{l}", _fm(inp["g_ffn_pre"][l]))
        put(f"g_ffn_post{l}", _fm(inp["g_ffn_post"][l]))
        fc = np.asarray(inp["f_conv_w"][l], np.float32)
        fc = fc.T.reshape(2 * FC, 128, 3).transpose(1, 0, 2).reshape(128, 2 * FC * 3)
        put(f"f_conv{l}", fc)
    ac = np.asarray(inp["a_conv_w"][0], np.float32).T.reshape(KC, 128, 3).transpose(1, 0, 2).reshape(128, KC * 3)
    put("a_conv", ac)
    put("b_b_pw1", _fm(inp["b_b_pw1"][0]))
    dw = np.asarray(inp["b_dw_w"][0], np.float32).T.reshape(KC, 128, CK).transpose(1, 0, 2).reshape(128, KC * CK)
    put("b_dw_w", dw)
    put("b_dw_b", _fm(inp["b_dw_b"][0]))
    put("b_ln_g", _fm(inp["b_ln_g"][0]))
    put("b_ln_b", _fm(inp["b_ln_b"][0]))
    put("b_b_pw2", _fm(inp["b_b_pw2"][0]))
    cs[:, OFF["mask"]] = mask_val
    cs[:, OFF["ident"]:OFF["ident"] + 128] = np.eye(128, dtype=np.float32)
    cs[:, OFF["ones"]:OFF["ones"] + 128] = 1.0
    return cs


T_TILE = 298
N_TILE = 7


def make_in_maps(inp, cores, T=T_TILE, NTILE=N_TILE):
    NT = T * NTILE
    HALO = NT - CHUNK
    x = np.asarray(inp["x"], np.float32)
    wmap = {
        "a_w_in": np.asarray(inp["a_w_in"][0], np.float32), "a_w_out": np.asarray(inp["a_w_out"][0], np.float32),
        "b_w_pw1": np.asarray(inp["b_w_pw1"][0], np.float32), "b_w_pw2": np.asarray(inp["b_w_pw2"][0], np.float32),
        "f_w_up0": np.asarray(inp["f_w_up"][0], np.float32), "f_w_up1": np.asarray(inp["f_w_up"][1], np.float32),
        "f_w_down0": np.asarray(inp["f_w_down"][0], np.float32),
        "f_w_down1": np.asarray(inp["f_w_down"][1], np.float32),
    }
    maps = []
    for c in cores:
        b, s = divmod(c, NCORE // BATCH)
        start = s * CHUNK
        xs = np.zeros((NT, D), np.float32)
        lo = start - HALO
        if lo >= 0:
            xs[:] = x[b, lo:start + CHUNK]
        else:
            xs[-lo:] = x[b, 0:start + CHUNK]
        m = dict(wmap)
        m["xs"] = xs
        m["consts"] = pack_consts(inp, 0.0 if s == 0 else 1.0)
        maps.append(m)
    return maps


def kernel(**inputs):
    T, NTILE = T_TILE, N_TILE
    HALO = T * NTILE - CHUNK
    nc = build_program(T, NTILE, NTILE, HALO)
    cores = list(range(NCORE))
    in_maps = make_in_maps(inputs, cores, T, NTILE)
    res = run_bass_kernel_spmd(nc, in_maps, core_ids=cores)
    out = np.empty((BATCH, SEQ, D), np.float32)
    for c in cores:
        b, s = divmod(c, NCORE // BATCH)
        out[b, s * CHUNK:(s + 1) * CHUNK] = res.results[c]["y"]
    return out
```

```python
import numpy as np
from contextlib import ExitStack

import concourse.bass as bass
import concourse.mybir as mybir
from concourse.bass_utils import run_bass_kernel_spmd

F32 = mybir.dt.float32
BF16 = mybir.dt.bfloat16
AF = mybir.ActivationFunctionType
ALU = mybir.AluOpType

D = 4096
KC = 32
DFF = 11008
FC = 86
HKC = 43
SEQ = 8192
BATCH = 2
NCORE = 8
CHUNK = 2048
RMS_EPS = 1e-6
LN_EPS = 1e-5
CK = 31

SAME_ENGINE_SYNC = False
NPOOL = 0

OFF = {}
_o = 0
for _name, _n in [("g_mix_pre0", 32), ("g_mix_pre1", 32), ("g_mix_post0", 32), ("g_mix_post1", 32),
                  ("g_ffn_pre0", 32), ("g_ffn_pre1", 32), ("g_ffn_post0", 32), ("g_ffn_post1", 32),
                  ("a_conv", 96), ("b_b_pw1", 64), ("b_dw_w", 32 * CK), ("b_dw_b", 32), ("b_ln_g", 32),
                  ("b_ln_b", 32), ("b_b_pw2", 32), ("f_conv0", 172 * 3), ("f_conv1", 172 * 3), ("mask", 1), ("ident", 128), ("ones", 128)]:
    OFF[_name] = _o
    _o += _n
NCONST = _o


def _merge(dst, src):
    for k, v in src.items():
        if dst.get(k, 0) < v:
            dst[k] = v


class Res:
    __slots__ = ("name", "lw", "rd")

    def __init__(self, name):
        self.name = name
        self.lw = {}
        self.rd = {}


class Region:
    def __init__(self, roles):
        self.roles = roles
        self.cur = None

    def use(self, role):
        if self.cur == role:
            return
        if self.cur is not None:
            lw, rd = {}, {}
            for r in self.roles[self.cur]:
                _merge(lw, r.lw)
                _merge(rd, r.rd)
            for r in self.roles[role]:
                r.lw = dict(lw)
                r.rd = dict(rd)
        self.cur = role


class Sched:
    def __init__(self, nc, es):
        self.nc = nc
        self.es = es
        self.engobj = {"pe": nc.tensor, "act": nc.scalar, "dve": nc.vector, "pool": nc.gpsimd, "sp": nc.sync}
        self.semh = {}
        self.cnt = {}
        self.waited = {e: {} for e in self.engobj}
        self.own = {}
        for e in ("pe", "act", "dve", "pool"):
            self.own[e] = self.newsem("e_" + e)

    def newsem(self, name):
        h = self.es.enter_context(self.nc.semaphore(name))
        self.semh[name] = h
        self.cnt[name] = 0
        return name

    def _deps(self, e, reads, writes):
        need = {}
        for r in reads:
            _merge(need, r.lw)
        for w in writes:
            _merge(need, w.lw)
            _merge(need, w.rd)
        own = self.own.get(e)
        wd = self.waited[e]
        eng = self.engobj[e]
        for k, v in need.items():
            if k == own and not SAME_ENGINE_SYNC:
                continue
            if wd.get(k, 0) >= v:
                continue
            eng.wait_ge(self.semh[k], v)
            wd[k] = v

    def _commit(self, k, v, reads, writes):
        for r in reads:
            if r.rd.get(k, 0) < v:
                r.rd[k] = v
        for w in writes:
            w.lw = {k: v}
            w.rd = {}

    def op(self, e, reads, writes, fn):
        self._deps(e, reads, writes)
        ins = fn(self.engobj[e])
        k = self.own[e]
        self.cnt[k] += 1
        ins.then_inc(self.semh[k], 1)
        self._commit(k, self.cnt[k], reads, writes)

    def group(self, e, reads, writes, fns):
        self._deps(e, reads, writes)
        ins = None
        eng = self.engobj[e]
        for fn in fns:
            ins = fn(eng)
        k = self.own[e]
        self.cnt[k] += 1
        ins.then_inc(self.semh[k], 1)
        self._commit(k, self.cnt[k], reads, writes)

    def dma(self, q, semk, out_ap, in_ap, reads, writes):
        self._deps(q, reads, writes)
        ins = self.engobj[q].dma_start(out=out_ap, in_=in_ap)
        self.cnt[semk] += 16
        ins.then_inc(self.semh[semk], 16)
        self._commit(semk, self.cnt[semk], reads, writes)

    def final_wait(self, e, res_list):
        need = {}
        for r in res_list:
            _merge(need, r.lw)
            _merge(need, r.rd)
        for k, v in need.items():
            self.engobj[e].wait_ge(self.semh[k], v)


def build_program(T, NTILE, NRUN, HALO, layers=(0, 1), nslot=3):
    NT = T * NTILE
    NOUT = NRUN * T - HALO
    NB = (T + 127) // 128
    nc = bass.Bass("TRN2", target_bir_lowering=False)
    x_d = nc.dram_tensor("xs", [NT, D], F32, kind="ExternalInput").ap()
    cs_d = nc.dram_tensor("consts", [128, NCONST], F32, kind="ExternalInput").ap()
    wd = {}
    for name, shp in [("a_w_in", [D, 3 * D]), ("a_w_out", [D, D]), ("b_w_pw1", [D, 2 * D]), ("b_w_pw2", [D, D]),
                      ("f_w_up0", [D, 2 * DFF]), ("f_w_up1", [D, 2 * DFF]),
                      ("f_w_down0", [DFF, D]), ("f_w_down1", [DFF, D])]:
        wd[name] = nc.dram_tensor(name, shp, F32, kind="ExternalInput").ap().rearrange("(kc p) n -> p kc n", p=128)
    y_d = nc.dram_tensor("y", [NOUT, D], F32, kind="ExternalOutput").ap()

    es = ExitStack()
    with es:
        S = Sched(nc, es)
        sb = lambda name, shape, dt: es.enter_context(nc.sbuf_tensor(name, shape, dt))
        cs = sb("cs", [128, NCONST], F32)
        xres = sb("xres", [128, KC, T], F32)
        bx = sb("bx", [128, KC * T], F32)
        hidr = sb("hidr", [128, FC * T], BF16)
        slots = [sb(f"wslot{i}", [128, HKC, 128], BF16) for i in range(nslot)]
        EW = T + CK - 1
        ext = [sb(f"ext{i}", [128, EW], F32) for i in range(4)]
        acc = [sb(f"acc{i}", [128, T], F32) for i in range(4)]
        tmp = [sb(f"tmp{i}", [128, T], F32) for i in range(4)]
        sqb = [sb(f"sq{i}", [128, T], BF16) for i in range(4)]
        onesb = sb("onesb", [128, 128], BF16)
        acc2 = ptmp = None
        bc = [sb(f"bc{i}", [128, T], F32) for i in range(3)]
        carA = sb("carA", [128, KC, 2], F32)
        carB = sb("carB", [128, KC, CK - 1], F32)
        carF = [sb(f"carF{l}", [128, 2 * FC, 2], F32) for l in range(2)]
        banks = [es.enter_context(nc.psum_tensor(f"bank{i}", [128, 512], F32)) for i in range(8)]

        ident = cs[:, OFF["ident"]:OFF["ident"] + 128]
        ones = cs[:, OFF["ones"]:OFF["ones"] + 128]
        bxb = bx[:].bitcast(BF16)
        hidf = hidr[:].bitcast(F32)
        xn = lambda k: bxb[:, k * T:(k + 1) * T]
        v1 = lambda m: bx[:, m * T:(m + 1) * T]
        hid = lambda f: hidr[:, f * T:(f + 1) * T]
        v2 = lambda m: hidf[:, m * T:(m + 1) * T]
        stg = lambda b: hidf[:, b * D:(b + 1) * D]
        assert NB * D <= FC * T // 2 and KC * T <= FC * T // 2

        R_cs = Res("cs")
        R_x = [Res(f"x{k}") for k in range(KC)]
        R_xn = [Res(f"xn{k}") for k in range(KC)]
        R_v1 = [Res(f"v1{k}") for k in range(KC)]
        R_hid = [Res(f"hid{k}") for k in range(FC)]
        R_v2 = [Res(f"v2{k}") for k in range(KC)]
        R_stg = [Res(f"stg{k}") for k in range(NB)]
        REG_bx = Region({"xn": R_xn, "v1": R_v1})
        REG_hid = Region({"hid": R_hid, "v2": R_v2, "stg": R_stg})
        R_slot = [Res(f"slot{i}") for i in range(nslot)]
        R_ext = [Res(f"ext{i}") for i in range(4)]
        R_acc = [Res(f"acc{i}") for i in range(4)]
        R_tmp = [Res(f"tmp{i}") for i in range(4)]
        R_sq = [Res(f"sq{i}") for i in range(4)]
        R_acc2 = [Res(f"acc2_{i}") for i in range(2)]
        R_ptmp = [Res(f"ptmp{i}") for i in range(2)]
        R_bc = [Res(f"bc{i}") for i in range(3)]
        R_carA = [Res(f"carA{k}") for k in range(KC)]
        R_carB = [Res(f"carB{k}") for k in range(KC)]
        R_carF = [[Res(f"carF{l}_{k}") for k in range(2 * FC)] for l in range(2)]
        R_const = R_cs
        R_bank = [Res(f"bank{i}") for i in range(8)]
        sem_slot = [S.newsem(f"d_slot{i}") for i in range(nslot)]
        sem_stg = [S.newsem(f"d_stg{i}") for i in range(NB)]
        sem_cs = S.newsem("d_cs")

        S.dma("sp", sem_cs, cs[:], cs_d[:], [], [R_cs])
        S.op("dve", [], R_carA, lambda e: e.memset(carA[:], 0.0))
        S.op("dve", [], R_carB, lambda e: e.memset(carB[:], 0.0))
        for l in range(2):
            S.op("dve", [], R_carF[l], lambda e, l=l: e.memset(carF[l][:], 0.0))

        S.op("dve", [R_cs], [R_cs], lambda e: e.tensor_copy(out=onesb[:], in_=ones))

        def C(name, j=0, n=1):
            o = OFF[name] + j
            return cs[:, o:o + n]

        wstate = {"n": 0, "tile": 0, "s": 0}
        NSLAB = 2 * (96 + 32 + 172 + 64) if len(layers) == 2 else 96 + 32 + 172 + 64 + 64
        use_cache = NRUN > 1
        if use_cache:
            WCH = 160
            wcs = [nc.dram_tensor(f"wcache{j}", [min(WCH, NSLAB - j * WCH), 128, HKC * 128], BF16,
                                  kind="Internal").ap() for j in range((NSLAB + WCH - 1) // WCH)]
            wcache = [wcs[i // WCH][i % WCH] for i in range(NSLAB)]
            R_wc = [Res(f"wc{i}") for i in range(NSLAB)]
            sem_st = [S.newsem(f"d_st{i}") for i in range(nslot)]

        def load_slab(wname, kc0, nk, col0):
            i = wstate["n"] % nslot
            wstate["n"] += 1
            sidx = wstate["s"]
            wstate["s"] += 1
            if wstate["tile"] == 0 or not use_cache:
                S.dma("pool", sem_slot[i], slots[i][:, 0:nk, :], wd[wname][:, kc0:kc0 + nk, col0:col0 + 128],
                      [], [R_slot[i]])
                if use_cache:
                    S.dma("sp", sem_st[i], wcache[sidx][:, 0:nk * 128],
                          slots[i][:, 0:nk, :].rearrange("p k n -> p (k n)"), [R_slot[i]], [R_wc[sidx]])
            else:
                S.dma("sp", sem_slot[i], slots[i][:, 0:nk, :].rearrange("p k n -> p (k n)"),
                      wcache[sidx][:, 0:nk * 128], [R_wc[sidx]], [R_slot[i]])
            return i

        bank_rr = {"n": 0}

        def next_bank():
            b = bank_rr["n"] % 6
            bank_rr["n"] += 1
            return b

        def mm_group(bank, slab_list, rhs_fn, rhs_res):
            ktot = sum(nk for _, nk in slab_list)
            kk = 0
            for (si, nk) in slab_list:
                fns = []
                k0 = kk
                for j in range(nk):
                    fns.append(lambda e, si=si, j=j, kk=kk: e.matmul(
                        banks[bank][:, 0:T], lhsT=slots[si][:, j, :], rhs=rhs_fn(kk),
                        start=(kk == 0), stop=(kk == ktot - 1)))
                    kk += 1
                S.group("pe", [R_slot[si]] + rhs_res[k0:k0 + nk], [R_bank[bank]], fns)
            pend_tick()

        sq_rr = {"n": 0}

        def stats_acc(sbank, src_ap, src_res, first, last, func=AF.Square, bias=None, via_act=True, defer=False):
            if via_act:
                i = sq_rr["n"] % 4
                sq_rr["n"] += 1
                if bias is None:
                    S.op("act", src_res, [R_sq[i]], lambda e: e.activation(out=sqb[i][:], in_=src_ap, func=func))
                else:
                    S.op("act", src_res + [R_cs], [R_sq[i]],
                         lambda e: e.activation(out=sqb[i][:], in_=src_ap, func=func, bias=bias))
                rhs, rres = sqb[i][:], [R_sq[i]]
            else:
                i = sq_rr["n"] % 4
                sq_rr["n"] += 1
                S.op("act", src_res, [R_sq[i]], lambda e: e.activation(out=sqb[i][:], in_=src_ap, func=AF.Copy))
                rhs, rres = sqb[i][:], [R_sq[i]]
            def emit():
                S.op("pe", rres + [R_const], [R_bank[sbank]],
                     lambda e: e.matmul(banks[sbank][:, 0:T], lhsT=onesb[:], rhs=rhs, start=first, stop=last))
            if defer:
                pend.append([0, emit])
            else:
                emit()

        pend = []
        pstate = {"delay": 1}

        def pend_tick():
            for p in pend:
                p[0] += 1
            while pend and pend[0][0] >= pstate["delay"]:
                pend.pop(0)[1]()

        def pend_flush():
            while pend:
                pend.pop(0)[1]()

        eps_t = sb("eps_t", [128, 2], F32)
        R_eps = Res("eps")
        S.op("dve", [], [R_eps], lambda e: e.memset(eps_t[:, 0:1], RMS_EPS))
        S.op("dve", [], [R_eps], lambda e: e.memset(eps_t[:, 1:2], LN_EPS))

        def prenorm(gname):
            pend_flush()
            REG_bx.use("xn")
            S.op("act", [R_bank[6], R_eps], [R_bc[0]],
                 lambda e: e.activation(out=bc[0][:], in_=banks[6][:, 0:T], func=AF.Sqrt,
                                        bias=eps_t[:, 0:1], scale=1.0 / D))
            S.op("dve", [R_bc[0]], [R_bc[0]], lambda e: e.reciprocal(bc[0][:], bc[0][:]))
            for k in range(KC):
                S.op("dve", [R_x[k], R_bc[0], R_cs], [R_xn[k]],
                     lambda e, k=k: e.scalar_tensor_tensor(out=xn(k), in0=xres[:, k, :], scalar=C(gname, k),
                                                           in1=bc[0][:], op0=ALU.mult, op1=ALU.mult))

        def postnorm(gname, o_fn, o_res, next_stats=True):
            pend_flush()
            S.op("act", [R_bank[7], R_eps], [R_bc[1]],
                 lambda e: e.activation(out=bc[1][:], in_=banks[7][:, 0:T], func=AF.Sqrt,
                                        bias=eps_t[:, 0:1], scale=1.0 / D))
            S.op("dve", [R_bc[1]], [R_bc[1]], lambda e: e.reciprocal(bc[1][:], bc[1][:]))
            for m in range(KC):
                i = m % 4
                S.op("dve", [o_res[m], R_bc[1], R_cs], [R_tmp[i]],
                     lambda e, m=m, i=i: e.scalar_tensor_tensor(out=tmp[i][:], in0=o_fn(m), scalar=C(gname, m),
                                                                in1=bc[1][:], op0=ALU.mult, op1=ALU.mult))
                S.op("dve", [R_tmp[i], R_x[m]], [R_x[m]],
                     lambda e, m=m, i=i: e.tensor_tensor(out=xres[:, m, :], in0=xres[:, m, :], in1=tmp[i][:],
                                                         op=ALU.add))
                if next_stats:
                    stats_acc(6, xres[:, m, :], [R_x[m]], m == 0, m == KC - 1)

        def out_proj(wname, nkc, rhs_fn, rhs_res, o_fn, o_res, bias_name=None):
            pieces = [(0, nkc)] if nkc <= HKC else [(0, HKC), (HKC, nkc - HKC)]
            for m in range(KC):
                sl = [(load_slab(wname, k0, nk, m * 128), nk) for (k0, nk) in pieces]
                b = next_bank()
                mm_group(b, sl, rhs_fn, rhs_res)
                if bias_name is None:
                    S.op("act", [R_bank[b]], [o_res[m]],
                         lambda e, m=m, b=b: e.activation(out=o_fn(m), in_=banks[b][:, 0:T], func=AF.Copy))
                    stats_acc(7, banks[b][:, 0:T], [R_bank[b]], m == 0, m == KC - 1, defer=True)
                else:
                    S.op("act", [R_bank[b], R_cs], [o_res[m]],
                         lambda e, m=m, b=b: e.activation(out=o_fn(m), in_=banks[b][:, 0:T], func=AF.Identity,
                                                          bias=C(bias_name, m)))
                    stats_acc(7, banks[b][:, 0:T], [R_bank[b]], m == 0, m == KC - 1, bias=C(bias_name, m), defer=True)

        ext_rr = {"n": 0}

        def conv3(src_bank, wname, widx, car, car_res, mask_halo):
            i = ext_rr["n"] % 4
            ext_rr["n"] += 1
            S.group("act", [car_res, R_bank[src_bank]], [R_ext[i]], [
                lambda e: e.activation(out=ext[i][:, 0:2], in_=car, func=AF.Copy),
                lambda e: e.activation(out=ext[i][:, 2:2 + T], in_=banks[src_bank][:, 0:T], func=AF.Copy)])
            if mask_halo:
                S.op("dve", [R_ext[i], R_cs], [R_ext[i]],
                     lambda e: e.tensor_scalar(out=ext[i][:, 2:2 + HALO], in0=ext[i][:, 2:2 + HALO],
                                               scalar1=C("mask"), scalar2=None, op0=ALU.mult))
            S.group("act", [R_ext[i], R_cs], [car_res, R_acc[i]], [
                lambda e: e.activation(out=car, in_=ext[i][:, T:T + 2], func=AF.Copy),
                lambda e: e.activation(out=acc[i][:], in_=ext[i][:, 0:T], func=AF.Copy,
                                       scale=C(wname, widx * 3 + 0))])
            for k in (1, 2):
                S.op("dve", [R_ext[i], R_acc[i], R_cs], [R_acc[i]],
                     lambda e, k=k: e.scalar_tensor_tensor(out=acc[i][:], in0=ext[i][:, k:k + T],
                                                           scalar=C(wname, widx * 3 + k), in1=acc[i][:],
                                                           op0=ALU.mult, op1=ALU.add))
            return i

        def ffn(l, first_tile, last_layer):
            prenorm(f"g_ffn_pre{l}")
            REG_hid.use("hid")
            wname = f"f_w_up{l}"
            cname = f"f_conv{l}"
            mask = first_tile and l == 1
            for f in range(FC):
                sg = load_slab(wname, 0, KC, f * 128)
                sv = load_slab(wname, 0, KC, DFF + f * 128)
                bg, bv = next_bank(), next_bank()
                mm_group(bg, [(sg, KC)], xn, R_xn)
                mm_group(bv, [(sv, KC)], xn, R_xn)
                ig = conv3(bg, cname, f, carF[l][:, f, :], R_carF[l][f], mask)
                iv = conv3(bv, cname, FC + f, carF[l][:, FC + f, :], R_carF[l][FC + f], mask)
                S.op("act", [R_acc[ig]], [R_tmp[ig]],
                     lambda e: e.activation(out=tmp[ig][:], in_=acc[ig][:], func=AF.Silu))
                S.op("dve", [R_tmp[ig], R_acc[iv]], [R_hid[f]],
                     lambda e, f=f: e.tensor_tensor(out=hid(f), in0=tmp[ig][:], in1=acc[iv][:], op=ALU.mult))
            REG_bx.use("v1")
            out_proj(f"f_w_down{l}", FC, hid, R_hid, v1, R_v1)
            postnorm(f"g_ffn_post{l}", v1, R_v1, next_stats=not last_layer)

        def mixer_a():
            prenorm("g_mix_pre0")
            REG_hid.use("hid")
            for m in range(KC):
                s_b = load_slab("a_w_in", 0, KC, m * 128)
                s_c = load_slab("a_w_in", 0, KC, D + m * 128)
                s_h = load_slab("a_w_in", 0, KC, 2 * D + m * 128)
                bb, bcn, bh = next_bank(), next_bank(), next_bank()
                mm_group(bb, [(s_b, KC)], xn, R_xn)
                mm_group(bcn, [(s_c, KC)], xn, R_xn)
                mm_group(bh, [(s_h, KC)], xn, R_xn)
                i = ext_rr["n"] % 4
                ext_rr["n"] += 1
                S.op("act", [R_bank[bcn]], [R_tmp[i]],
                     lambda e: e.activation(out=tmp[i][:], in_=banks[bcn][:, 0:T], func=AF.Copy))
                S.op("act", [R_carA[m]], [R_ext[i]],
                     lambda e: e.activation(out=ext[i][:, 0:2], in_=carA[:, m, :], func=AF.Copy))
                S.op("dve", [R_bank[bh], R_tmp[i]], [R_ext[i]],
                     lambda e: e.tensor_tensor(out=ext[i][:, 2:2 + T], in0=banks[bh][:, 0:T], in1=tmp[i][:],
                                               op=ALU.mult))
                S.group("act", [R_ext[i], R_cs], [R_carA[m], R_acc[i]], [
                    lambda e: e.activation(out=carA[:, m, :], in_=ext[i][:, T:T + 2], func=AF.Copy),
                    lambda e: e.activation(out=acc[i][:], in_=ext[i][:, 0:T], func=AF.Copy,
                                           scale=C("a_conv", m * 3 + 0))])
                for k in (1, 2):
                    S.op("dve", [R_ext[i], R_acc[i], R_cs], [R_acc[i]],
                         lambda e, k=k: e.scalar_tensor_tensor(out=acc[i][:], in0=ext[i][:, k:k + T],
                                                               scalar=C("a_conv", m * 3 + k), in1=acc[i][:],
                                                               op0=ALU.mult, op1=ALU.add))
                S.op("dve", [R_bank[bb], R_acc[i]], [R_hid[m]],
                     lambda e: e.tensor_tensor(out=hid(m), in0=banks[bb][:, 0:T], in1=acc[i][:], op=ALU.mult))
            REG_bx.use("v1")
            out_proj("a_w_out", KC, hid, R_hid[0:KC], v1, R_v1)
            postnorm("g_mix_post0", v1, R_v1)

        def mixer_b(first_tile, use_pool):
            prenorm("g_mix_pre1")
            pstate["delay"] = 4
            REG_hid.use("v2")
            NK = CK - 1
            for m in range(KC):
                s_v = load_slab("b_w_pw1", 0, KC, m * 128)
                s_g = load_slab("b_w_pw1", 0, KC, D + m * 128)
                bv, bg = next_bank(), next_bank()
                mm_group(bv, [(s_v, KC)], xn, R_xn)
                mm_group(bg, [(s_g, KC)], xn, R_xn)
                i = ext_rr["n"] % 4
                ext_rr["n"] += 1
                S.op("act", [R_bank[bg], R_cs], [R_tmp[i]],
                     lambda e: e.activation(out=tmp[i][:], in_=banks[bg][:, 0:T], func=AF.Sigmoid,
                                            bias=C("b_b_pw1", KC + m)))
                S.op("act", [R_carB[m]], [R_ext[i]],
                     lambda e: e.activation(out=ext[i][:, 0:NK], in_=carB[:, m, :], func=AF.Copy))
                S.op("dve", [R_bank[bv], R_tmp[i], R_cs], [R_ext[i]],
                     lambda e: e.scalar_tensor_tensor(out=ext[i][:, NK:NK + T], in0=banks[bv][:, 0:T],
                                                      scalar=C("b_b_pw1", m), in1=tmp[i][:],
                                                      op0=ALU.add, op1=ALU.mult))
                if first_tile:
                    S.op("dve", [R_ext[i], R_cs], [R_ext[i]],
                         lambda e: e.tensor_scalar(out=ext[i][:, NK:NK + HALO], in0=ext[i][:, NK:NK + HALO],
                                                   scalar1=C("mask"), scalar2=None, op0=ALU.mult))
                S.group("act", [R_ext[i], R_cs], [R_carB[m], R_acc[i]], [
                    lambda e: e.activation(out=carB[:, m, :], in_=ext[i][:, T:T + NK], func=AF.Copy),
                    lambda e: e.activation(out=acc[i][:], in_=ext[i][:, 0:T], func=AF.Identity,
                                           scale=C("b_dw_w", m * CK + 0), bias=C("b_dw_b", m))])
                NP = NPOOL if use_pool else 0
                if NP:
                    j2 = m % 2
                    S.op("act", [R_ext[i], R_cs], [R_acc2[j2]],
                         lambda e: e.activation(out=acc2[j2][:], in_=ext[i][:, 1:1 + T], func=AF.Copy,
                                                scale=C("b_dw_w", m * CK + 1)))
                    for k in range(2, NP + 1):
                        jp = k % 2
                        S.op("act", [R_ext[i], R_cs], [R_ptmp[jp]],
                             lambda e, k=k, jp=jp: e.activation(out=ptmp[jp][:], in_=ext[i][:, k:k + T], func=AF.Copy,
                                                                scale=C("b_dw_w", m * CK + k)))
                        S.op("pool", [R_ptmp[jp], R_acc2[j2]], [R_acc2[j2]],
                             lambda e, jp=jp: e.tensor_tensor(out=acc2[j2][:], in0=acc2[j2][:], in1=ptmp[jp][:],
                                                              op=ALU.add))
                for k in range(NP + 1, CK):
                    last = (k == CK - 1) and not NP
                    S.op("dve", [R_ext[i], R_acc[i], R_cs], [R_v2[m]] if last else [R_acc[i]],
                         lambda e, k=k, last=last: e.scalar_tensor_tensor(
                             out=(v2(m) if last else acc[i][:]), in0=ext[i][:, k:k + T],
                             scalar=C("b_dw_w", m * CK + k), in1=acc[i][:], op0=ALU.mult, op1=ALU.add))
                if NP:
                    S.op("dve", [R_acc[i], R_acc2[j2]], [R_v2[m]],
                         lambda e: e.tensor_tensor(out=v2(m), in0=acc[i][:], in1=acc2[j2][:], op=ALU.add))
                stats_acc(6, v2(m), [R_v2[m]], m == 0, m == KC - 1, via_act=False, defer=True)
                stats_acc(7, v2(m), [R_v2[m]], m == 0, m == KC - 1, defer=True)
            pend_flush()
            pstate["delay"] = 1
            S.op("act", [R_bank[6]], [R_bc[0]],
                 lambda e: e.activation(out=bc[0][:], in_=banks[6][:, 0:T], func=AF.Copy, scale=1.0 / D))
            S.op("dve", [R_bc[0]], [R_bc[2]],
                 lambda e: e.tensor_tensor(out=bc[2][:], in0=bc[0][:], in1=bc[0][:], op=ALU.mult))
            S.op("dve", [R_bank[7], R_bc[2]], [R_bc[2]],
                 lambda e: e.scalar_tensor_tensor(out=bc[2][:], in0=banks[7][:, 0:T], scalar=1.0 / D, in1=bc[2][:],
                                                  op0=ALU.mult, op1=ALU.subtract))
            S.op("act", [R_bc[2], R_eps], [R_bc[2]],
                 lambda e: e.activation(out=bc[2][:], in_=bc[2][:], func=AF.Sqrt, bias=eps_t[:, 1:2], scale=1.0))
            S.op("dve", [R_bc[2]], [R_bc[2]], lambda e: e.reciprocal(bc[2][:], bc[2][:]))
            for m in range(KC):
                i = m % 4
                S.op("dve", [R_v2[m], R_bc[0]], [R_tmp[i]],
                     lambda e: e.tensor_tensor(out=tmp[i][:], in0=v2(m), in1=bc[0][:], op=ALU.subtract))
                S.op("dve", [R_tmp[i], R_bc[2]], [R_tmp[i]],
                     lambda e: e.tensor_tensor(out=tmp[i][:], in0=tmp[i][:], in1=bc[2][:], op=ALU.mult))
                S.op("act", [R_tmp[i], R_cs], [R_xn[m]],
                     lambda e: e.activation(out=xn(m), in_=tmp[i][:], func=AF.Silu,
                                            scale=C("b_ln_g", m), bias=C("b_ln_b", m)))
            out_proj("b_w_pw2", KC, xn, R_xn, v2, R_v2, bias_name="b_b_pw2")
            postnorm("g_mix_post1", v2, R_v2)

        for it in range(NRUN):
            t0 = it * T
            wstate["tile"] = it
            wstate["s"] = 0
            REG_hid.use("stg")
            for b in range(NB):
                n = min(128, T - b * 128)
                S.dma("pool", sem_stg[b], stg(b)[0:n, :], x_d[t0 + b * 128:t0 + b * 128 + n, :], [], [R_stg[b]])
            for c in range(KC):
                bk = next_bank()
                fns = []
                for b in range(NB):
                    n = min(128, T - b * 128)
                    fns.append(lambda e, b=b, n=n: e.transpose(
                        out=banks[bk][:, b * 128:b * 128 + n], in_=stg(b)[0:n, c * 128:(c + 1) * 128],
                        identity=ident[0:n, 0:n]))
                S.group("pe", R_stg + [R_const], [R_bank[bk]], fns)
                eng = "act" if c % 2 == 0 else "dve"
                if eng == "act":
                    S.op("act", [R_bank[bk]], [R_x[c]],
                         lambda e: e.activation(out=xres[:, c, :], in_=banks[bk][:, 0:T], func=AF.Copy))
                else:
                    S.op("dve", [R_bank[bk]], [R_x[c]],
                         lambda e: e.tensor_copy(out=xres[:, c, :], in_=banks[bk][:, 0:T]))
                stats_acc(6, xres[:, c, :], [R_x[c]], c == 0, c == KC - 1)
            for l in layers:
                if l == 0:
                    mixer_a()
                else:
                    mixer_b(it == 0, use_cache and it >= 1)
                ffn(l, it == 0, l == layers[-1])
            REG_hid.use("stg")
            for b in range(NB):
                n = min(128, T - b * 128)
                lo = max(0, HALO - (t0 + b * 128))
                if lo >= n:
                    continue
                for c4 in range(KC // 4):
                    bk = next_bank()
                    fns = []
                    for j in range(4):
                        c = c4 * 4 + j
                        fns.append(lambda e, c=c, j=j: e.transpose(
                            out=banks[bk][0:n, j * 128:(j + 1) * 128], in_=xres[:, c, b * 128:b * 128 + n],
                            identity=ident))
                    S.group("pe", R_x[c4 * 4:c4 * 4 + 4] + [R_const], [R_bank[bk]], fns)
                    if c4 % 2 == 0:
                        S.op("act", [R_bank[bk]], [R_stg[b]],
                             lambda e: e.activation(out=stg(b)[0:n, c4 * 512:(c4 + 1) * 512], in_=banks[bk][0:n, :],
                                                    func=AF.Copy))
                    else:
                        S.op("dve", [R_bank[bk]], [R_stg[b]],
                             lambda e: e.tensor_copy(out=stg(b)[0:n, c4 * 512:(c4 + 1) * 512], in_=banks[bk][0:n, :]))
                g0 = t0 + b * 128 + lo - HALO
                S.dma("pool", sem_stg[b], y_d[g0:g0 + (n - lo), :], stg(b)[lo:n, :], [R_stg[b]], [])
        S.final_wait("pool", R_stg)
    return nc


def _fm(v):
    v = np.asarray(v, np.float32)
    return np.ascontiguousarray(v.reshape(-1, 128).T)


def pack_consts(inp, mask_val):
    cs = np.zeros((128, NCONST), np.float32)

    def put(name, arr):
        arr = np.asarray(arr, np.float32)
        cs[:, OFF[name]:OFF[name] + arr.shape[1]] = arr

    for l in range(2):
        put(f"g_mix_pre{l}", _fm(inp["g_mix_pre"][l]))
        put(f"g_mix_post{l}", _fm(inp["g_mix_post"][l]))
        put(f"g_ffn_pre{l}", _fm(inp["g_ffn_pre"][l]))
        put(f"g_ffn_post{l}", _fm(inp["g_ffn_post"][l]))
        fc = np.asarray(inp["f_conv_w"][l], np.float32)
        fc = fc.T.reshape(2 * FC, 128, 3).transpose(1, 0, 2).reshape(128, 2 * FC * 3)
        put(f"f_conv{l}", fc)
    ac = np.asarray(inp["a_conv_w"][0], np.float32).T.reshape(KC, 128, 3).transpose(1, 0, 2).reshape(128, KC * 3)
    put("a_conv", ac)
    put("b_b_pw1", _fm(inp["b_b_pw1"][0]))
    dw = np.asarray(inp["b_dw_w"][0], np.float32).T.reshape(KC, 128, CK).transpose(1, 0, 2).reshape(128, KC * CK)
    put("b_dw_w", dw)
    put("b_dw_b", _fm(inp["b_dw_b"][0]))
    put("b_ln_g", _fm(inp["b_ln_g"][0]))
    put("b_ln_b", _fm(inp["b_ln_b"][0]))
    put("b_b_pw2", _fm(inp["b_b_pw2"][0]))
    cs[:, OFF["mask"]] = mask_val
    cs[:, OFF["ident"]:OFF["ident"] + 128] = np.eye(128, dtype=np.float32)
    cs[:, OFF["ones"]:OFF["ones"] + 128] = 1.0
    return cs


T_TILE = 298
N_TILE = 7


def make_in_maps(inp, cores, T=T_TILE, NTILE=N_TILE):
    NT = T * NTILE
    HALO = NT - CHUNK
    x = np.asarray(inp["x"], np.float32)
    wmap = {
        "a_w_in": np.asarray(inp["a_w_in"][0], np.float32), "a_w_out": np.asarray(inp["a_w_out"][0], np.float32),
        "b_w_pw1": np.asarray(inp["b_w_pw1"][0], np.float32), "b_w_pw2": np.asarray(inp["b_w_pw2"][0], np.float32),
        "f_w_up0": np.asarray(inp["f_w_up"][0], np.float32), "f_w_up1": np.asarray(inp["f_w_up"][1], np.float32),
        "f_w_down0": np.asarray(inp["f_w_down"][0], np.float32),
        "f_w_down1": np.asarray(inp["f_w_down"][1], np.float32),
    }
    maps = []
    for c in cores:
        b, s = divmod(c, NCORE // BATCH)
        start = s * CHUNK
        xs = np.zeros((NT, D), np.float32)
        lo = start - HALO
        if lo >= 0:
            xs[:] = x[b, lo:start + CHUNK]
        else:
            xs[-lo:] = x[b, 0:start + CHUNK]
        m = dict(wmap)
        m["xs"] = xs
        m["consts"] = pack_consts(inp, 0.0 if s == 0 else 1.0)
        maps.append(m)
    return maps


def kernel(**inputs):
    T, NTILE = T_TILE, N_TILE
    HALO = T * NTILE - CHUNK
    nc = build_program(T, NTILE, NTILE, HALO)
    cores = list(range(NCORE))
    in_maps = make_in_maps(inputs, cores, T, NTILE)
    res = run_bass_kernel_spmd(nc, in_maps, core_ids=cores)
    out = np.empty((BATCH, SEQ, D), np.float32)
    for c in cores:
        b, s = divmod(c, NCORE // BATCH)
        out[b, s * CHUNK:(s + 1) * CHUNK] = res.results[c]["y"]
    return out
```

```python
import numpy as np
from contextlib import ExitStack

import concourse.bass as bass
import concourse.mybir as mybir
from concourse.bass_utils import run_bass_kernel_spmd

F32 = mybir.dt.float32
BF16 = mybir.dt.bfloat16
AF = mybir.ActivationFunctionType
ALU = mybir.AluOpType

D = 4096
KC = 32
DFF = 11008
FC = 86
HKC = 43
SEQ = 8192
BATCH = 2
NCORE = 8
CHUNK = 2048
RMS_EPS = 1e-6
LN_EPS = 1e-5
CK = 31

SAME_ENGINE_SYNC = False
NPOOL = 0

OFF = {}
_o = 0
for _name, _n in [("g_mix_pre0", 32), ("g_mix_pre1", 32), ("g_mix_post0", 32), ("g_mix_post1", 32),
                  ("g_ffn_pre0", 32), ("g_ffn_pre1", 32), ("g_ffn_post0", 32), ("g_ffn_post1", 32),
                  ("a_conv", 96), ("b_b_pw1", 64), ("b_dw_w", 32 * CK), ("b_dw_b", 32), ("b_ln_g", 32),
                  ("b_ln_b", 32), ("b_b_pw2", 32), ("f_conv0", 172 * 3), ("f_conv1", 172 * 3), ("mask", 1), ("ident", 128), ("ones", 128)]:
    OFF[_name] = _o
    _o += _n
NCONST = _o


def _merge(dst, src):
    for k, v in src.items():
        if dst.get(k, 0) < v:
            dst[k] = v


class Res:
    __slots__ = ("name", "lw", "rd")

    def __init__(self, name):
        self.name = name
        self.lw = {}
        self.rd = {}


class Region:
    def __init__(self, roles):
        self.roles = roles
        self.cur = None

    def use(self, role):
        if self.cur == role:
            return
        if self.cur is not None:
            lw, rd = {}, {}
            for r in self.roles[self.cur]:
                _merge(lw, r.lw)
                _merge(rd, r.rd)
            for r in self.roles[role]:
                r.lw = dict(lw)
                r.rd = dict(rd)
        self.cur = role


class Sched:
    def __init__(self, nc, es):
        self.nc = nc
        self.es = es
        self.engobj = {"pe": nc.tensor, "act": nc.scalar, "dve": nc.vector, "pool": nc.gpsimd, "sp": nc.sync}
        self.semh = {}
        self.cnt = {}
        self.waited = {e: {} for e in self.engobj}
        self.own = {}
        for e in ("pe", "act", "dve", "pool"):
            self.own[e] = self.newsem("e_" + e)

    def newsem(self, name):
        h = self.es.enter_context(self.nc.semaphore(name))
        self.semh[name] = h
        self.cnt[name] = 0
        return name

    def _deps(self, e, reads, writes, skip_own=True):
        need = {}
        for r in reads:
            _merge(need, r.lw)
        for w in writes:
            _merge(need, w.lw)
            _merge(need, w.rd)
        own = self.own.get(e)
        wd = self.waited[e]
        eng = self.engobj[e]
        for k, v in need.items():
            if k == own and skip_own and not SAME_ENGINE_SYNC:
                continue
            if wd.get(k, 0) >= v:
                continue
            eng.wait_ge(self.semh[k], v)
            wd[k] = v

    def _commit(self, k, v, reads, writes):
        for r in reads:
            if r.rd.get(k, 0) < v:
                r.rd[k] = v
        for w in writes:
            w.lw = {k: v}
            w.rd = {}

    def op(self, e, reads, writes, fn):
        self._deps(e, reads, writes)
        ins = fn(self.engobj[e])
        k = self.own[e]
        self.cnt[k] += 1
        ins.then_inc(self.semh[k], 1)
        self._commit(k, self.cnt[k], reads, writes)

    def group(self, e, reads, writes, fns):
        self._deps(e, reads, writes)
        ins = None
        eng = self.engobj[e]
        for fn in fns:
            ins = fn(eng)
        k = self.own[e]
        self.cnt[k] += 1
        ins.then_inc(self.semh[k], 1)
        self._commit(k, self.cnt[k], reads, writes)

    def dma(self, q, semk, out_ap, in_ap, reads, writes):
        self._deps(q, reads, writes, skip_own=False)
        ins = self.engobj[q].dma_start(out=out_ap, in_=in_ap)
        self.cnt[semk] += 16
        ins.then_inc(self.semh[semk], 16)
        self._commit(semk, self.cnt[semk], reads, writes)

    def final_wait(self, e, res_list):
        need = {}
        for r in res_list:
            _merge(need, r.lw)
            _merge(need, r.rd)
        for k, v in need.items():
            self.engobj[e].wait_ge(self.semh[k], v)


def build_program(T, NTILE, NRUN, HALO, layers=(0, 1), nslot=3):
    NT = T * NTILE
    NOUT = NRUN * T - HALO
    NB = (T + 127) // 128
    nc = bass.Bass("TRN2", target_bir_lowering=False)
    x_d = nc.dram_tensor("xs", [NT, D], F32, kind="ExternalInput").ap()
    cs_d = nc.dram_tensor("consts", [128, NCONST], F32, kind="ExternalInput").ap()
    wd = {}
    for name, shp in [("a_w_in", [D, 3 * D]), ("a_w_out", [D, D]), ("b_w_pw1", [D, 2 * D]), ("b_w_pw2", [D, D]),
                      ("f_w_up0", [D, 2 * DFF]), ("f_w_up1", [D, 2 * DFF]),
                      ("f_w_down0", [DFF, D]), ("f_w_down1", [DFF, D])]:
        wd[name] = nc.dram_tensor(name, shp, F32, kind="ExternalInput").ap().rearrange("(kc p) n -> p kc n", p=128)
    y_d = nc.dram_tensor("y", [NOUT, D], F32, kind="ExternalOutput").ap()

    es = ExitStack()
    with es:
        S = Sched(nc, es)
        sb = lambda name, shape, dt: es.enter_context(nc.sbuf_tensor(name, shape, dt))
        cs = sb("cs", [128, NCONST], F32)
        xres = sb("xres", [128, KC, T], F32)
        bx = sb("bx", [128, KC * T], F32)
        hidr = sb("hidr", [128, FC * T], BF16)
        slots = [sb(f"wslot{i}", [128, HKC, 128], BF16) for i in range(nslot)]
        EW = T + CK - 1
        ext = [sb(f"ext{i}", [128, EW], F32) for i in range(4)]
        acc = [sb(f"acc{i}", [128, T], F32) for i in range(4)]
        tmp = [sb(f"tmp{i}", [128, T], F32) for i in range(4)]
        sqb = [sb(f"sq{i}", [128, T], BF16) for i in range(4)]
        onesb = sb("onesb", [128, 128], BF16)
        acc2 = ptmp = None
        bc = [sb(f"bc{i}", [128, T], F32) for i in range(3)]
        carA = sb("carA", [128, KC, 2], F32)
        carB = sb("carB", [128, KC, CK - 1], F32)
        carF = [sb(f"carF{l}", [128, 2 * FC, 2], F32) for l in range(2)]
        banks = [es.enter_context(nc.psum_tensor(f"bank{i}", [128, 512], F32)) for i in range(8)]

        ident = cs[:, OFF["ident"]:OFF["ident"] + 128]
        ones = cs[:, OFF["ones"]:OFF["ones"] + 128]
        bxb = bx[:].bitcast(BF16)
        hidf = hidr[:].bitcast(F32)
        xn = lambda k: bxb[:, k * T:(k + 1) * T]
        v1 = lambda m: bx[:, m * T:(m + 1) * T]
        hid = lambda f: hidr[:, f * T:(f + 1) * T]
        v2 = lambda m: hidf[:, m * T:(m + 1) * T]
        stg = lambda b: hidf[:, b * D:(b + 1) * D]
        assert NB * D <= FC * T // 2 and KC * T <= FC * T // 2

        R_cs = Res("cs")
        R_x = [Res(f"x{k}") for k in range(KC)]
        R_xn = [Res(f"xn{k}") for k in range(KC)]
        R_v1 = [Res(f"v1{k}") for k in range(KC)]
        R_hid = [Res(f"hid{k}") for k in range(FC)]
        R_v2 = [Res(f"v2{k}") for k in range(KC)]
        R_stg = [Res(f"stg{k}") for k in range(NB)]
        REG_bx = Region({"xn": R_xn, "v1": R_v1})
        REG_hid = Region({"hid": R_hid, "v2": R_v2, "stg": R_stg})
        R_slot = [Res(f"slot{i}") for i in range(nslot)]
        R_ext = [Res(f"ext{i}") for i in range(4)]
        R_acc = [Res(f"acc{i}") for i in range(4)]
        R_tmp = [Res(f"tmp{i}") for i in range(4)]
        R_sq = [Res(f"sq{i}") for i in range(4)]
        R_acc2 = [Res(f"acc2_{i}") for i in range(2)]
        R_ptmp = [Res(f"ptmp{i}") for i in range(2)]
        R_bc = [Res(f"bc{i}") for i in range(3)]
        R_carA = [Res(f"carA{k}") for k in range(KC)]
        R_carB = [Res(f"carB{k}") for k in range(KC)]
        R_carF = [[Res(f"carF{l}_{k}") for k in range(2 * FC)] for l in range(2)]
        R_const = R_cs
        R_bank = [Res(f"bank{i}") for i in range(8)]
        sem_slot = [S.newsem(f"d_slot{i}") for i in range(nslot)]
        sem_stg = [S.newsem(f"d_stg{i}") for i in range(NB)]
        sem_cs = S.newsem("d_cs")

        S.dma("sp", sem_cs, cs[:], cs_d[:], [], [R_cs])
        S.op("dve", [], R_carA, lambda e: e.memset(carA[:], 0.0))
        S.op("dve", [], R_carB, lambda e: e.memset(carB[:], 0.0))
        for l in range(2):
            S.op("dve", [], R_carF[l], lambda e, l=l: e.memset(carF[l][:], 0.0))

        S.op("dve", [R_cs], [R_cs], lambda e: e.tensor_copy(out=onesb[:], in_=ones))

        def C(name, j=0, n=1):
            o = OFF[name] + j
            return cs[:, o:o + n]

        wstate = {"n": 0, "tile": 0, "s": 0}
        NSLAB = 2 * (96 + 32 + 172 + 64) if len(layers) == 2 else 96 + 32 + 172 + 64 + 64
        use_cache = NRUN > 1
        if use_cache:
            WCH = 160
            wcs = [nc.dram_tensor(f"wcache{j}", [min(WCH, NSLAB - j * WCH), 128, HKC * 128], BF16,
                                  kind="Internal").ap() for j in range((NSLAB + WCH - 1) // WCH)]
            wcache = [wcs[i // WCH][i % WCH] for i in range(NSLAB)]
            R_wc = [Res(f"wc{i}") for i in range(NSLAB)]
            sem_st = [S.newsem(f"d_st{i}") for i in range(nslot)]

        NS0 = 96 + 32 + 172 + 64
        LM = use_cache and tuple(layers) == (0, 1)
        bg_list = []
        if LM:
            for m in range(KC):
                bg_list.append(("b_w_pw1", 0, KC, m * 128))
                bg_list.append(("b_w_pw1", 0, KC, D + m * 128))
            for m in range(KC):
                bg_list.append(("b_w_pw2", 0, KC, m * 128))
            for f in range(FC):
                bg_list.append(("f_w_up1", 0, KC, f * 128))
                bg_list.append(("f_w_up1", 0, KC, DFF + f * 128))
            for m in range(KC):
                bg_list.append(("f_w_down1", 0, HKC, m * 128))
                bg_list.append(("f_w_down1", HKC, FC - HKC, m * 128))
            NBG = 4
            sem_bg = [S.newsem(f"d_bg{i}") for i in range(NBG)]
            R_bgr = [Res(f"bgr{i}") for i in range(NBG)]
        bgstate = {"next": 0, "on": False, "cnt": 0, "stride": max(1, NRUN - 1)}

        def bg_convert_one():
            j = bgstate["next"]
            if j >= len(bg_list):
                return
            bgstate["next"] += 1
            wname, kc0, nk, col0 = bg_list[j]
            sidx = NS0 + j
            r = j % NBG
            S.dma("pool", sem_bg[r], wcache[sidx][:, 0:nk * 128].rearrange("p (k n) -> p k n", n=128),
                  wd[wname][:, kc0:kc0 + nk, col0:col0 + 128], [], [R_wc[sidx], R_bgr[r]])

        def load_slab(wname, kc0, nk, col0):
            i = wstate["n"] % nslot
            wstate["n"] += 1
            sidx = wstate["s"]
            wstate["s"] += 1
            if LM and sidx >= NS0:
                assert bg_list[sidx - NS0] == (wname, kc0, nk, col0), (sidx, wname, kc0, nk, col0)
            from_cache = use_cache and (wstate["tile"] > 0 or (LM and sidx >= NS0))
            if not from_cache:
                S.dma("pool", sem_slot[i], slots[i][:, 0:nk, :], wd[wname][:, kc0:kc0 + nk, col0:col0 + 128],
                      [], [R_slot[i]])
                if use_cache:
                    S.dma("sp", sem_st[i], wcache[sidx][:, 0:nk * 128],
                          slots[i][:, 0:nk, :].rearrange("p k n -> p (k n)"), [R_slot[i]], [R_wc[sidx]])
            else:
                S.dma("sp", sem_slot[i], slots[i][:, 0:nk, :].rearrange("p k n -> p (k n)"),
                      wcache[sidx][:, 0:nk * 128], [R_wc[sidx]], [R_slot[i]])
            return i

        bank_rr = {"n": 0}

        def next_bank():
            b = bank_rr["n"] % 6
            bank_rr["n"] += 1
            return b

        def mm_group(bank, slab_list, rhs_fn, rhs_res):
            ktot = sum(nk for _, nk in slab_list)
            kk = 0
            for (si, nk) in slab_list:
                fns = []
                k0 = kk
                for j in range(nk):
                    fns.append(lambda e, si=si, j=j, kk=kk: e.matmul(
                        banks[bank][:, 0:T], lhsT=slots[si][:, j, :], rhs=rhs_fn(kk),
                        start=(kk == 0), stop=(kk == ktot - 1)))
                    kk += 1
                S.group("pe", [R_slot[si]] + rhs_res[k0:k0 + nk], [R_bank[bank]], fns)
            pend_tick()
            if bgstate["on"]:
                bgstate["cnt"] += 1
                if bgstate["cnt"] % bgstate["stride"] == 0:
                    bg_convert_one()

        sq_rr = {"n": 0}

        def stats_acc(sbank, src_ap, src_res, first, last, func=AF.Square, bias=None, via_act=True, defer=False):
            if via_act:
                i = sq_rr["n"] % 4
                sq_rr["n"] += 1
                if bias is None:
                    S.op("act", src_res, [R_sq[i]], lambda e: e.activation(out=sqb[i][:], in_=src_ap, func=func))
                else:
                    S.op("act", src_res + [R_cs], [R_sq[i]],
                         lambda e: e.activation(out=sqb[i][:], in_=src_ap, func=func, bias=bias))
                rhs, rres = sqb[i][:], [R_sq[i]]
            else:
                i = sq_rr["n"] % 4
                sq_rr["n"] += 1
                S.op("act", src_res, [R_sq[i]], lambda e: e.activation(out=sqb[i][:], in_=src_ap, func=AF.Copy))
                rhs, rres = sqb[i][:], [R_sq[i]]
            def emit():
                S.op("pe", rres + [R_const], [R_bank[sbank]],
                     lambda e: e.matmul(banks[sbank][:, 0:T], lhsT=onesb[:], rhs=rhs, start=first, stop=last))
            if defer:
                pend.append([0, emit])
            else:
                emit()

        pend = []
        pstate = {"delay": 1}

        def pend_tick():
            for p in pend:
                p[0] += 1
            while pend and pend[0][0] >= pstate["delay"]:
                pend.pop(0)[1]()

        def pend_flush():
            while pend:
                pend.pop(0)[1]()

        eps_t = sb("eps_t", [128, 2], F32)
        R_eps = Res("eps")
        S.op("dve", [], [R_eps], lambda e: e.memset(eps_t[:, 0:1], RMS_EPS))
        S.op("dve", [], [R_eps], lambda e: e.memset(eps_t[:, 1:2], LN_EPS))

        def prenorm(gname):
            pend_flush()
            REG_bx.use("xn")
            S.op("act", [R_bank[6], R_eps], [R_bc[0]],
                 lambda e: e.activation(out=bc[0][:], in_=banks[6][:, 0:T], func=AF.Sqrt,
                                        bias=eps_t[:, 0:1], scale=1.0 / D))
            S.op("dve", [R_bc[0]], [R_bc[0]], lambda e: e.reciprocal(bc[0][:], bc[0][:]))
            for k in range(KC):
                S.op("dve", [R_x[k], R_bc[0], R_cs], [R_xn[k]],
                     lambda e, k=k: e.scalar_tensor_tensor(out=xn(k), in0=xres[:, k, :], scalar=C(gname, k),
                                                           in1=bc[0][:], op0=ALU.mult, op1=ALU.mult))

        def postnorm(gname, o_fn, o_res, next_stats=True):
            pend_flush()
            S.op("act", [R_bank[7], R_eps], [R_bc[1]],
                 lambda e: e.activation(out=bc[1][:], in_=banks[7][:, 0:T], func=AF.Sqrt,
                                        bias=eps_t[:, 0:1], scale=1.0 / D))
            S.op("dve", [R_bc[1]], [R_bc[1]], lambda e: e.reciprocal(bc[1][:], bc[1][:]))
            for m in range(KC):
                i = m % 4
                S.op("dve", [o_res[m], R_bc[1], R_cs], [R_tmp[i]],
                     lambda e, m=m, i=i: e.scalar_tensor_tensor(out=tmp[i][:], in0=o_fn(m), scalar=C(gname, m),
                                                                in1=bc[1][:], op0=ALU.mult, op1=ALU.mult))
                S.op("dve", [R_tmp[i], R_x[m]], [R_x[m]],
                     lambda e, m=m, i=i: e.tensor_tensor(out=xres[:, m, :], in0=xres[:, m, :], in1=tmp[i][:],
                                                         op=ALU.add))
                if next_stats:
                    stats_acc(6, xres[:, m, :], [R_x[m]], m == 0, m == KC - 1)

        def out_proj(wname, nkc, rhs_fn, rhs_res, o_fn, o_res, bias_name=None):
            pieces = [(0, nkc)] if nkc <= HKC else [(0, HKC), (HKC, nkc - HKC)]
            for m in range(KC):
                sl = [(load_slab(wname, k0, nk, m * 128), nk) for (k0, nk) in pieces]
                b = next_bank()
                mm_group(b, sl, rhs_fn, rhs_res)
                if bias_name is None:
                    S.op("act", [R_bank[b]], [o_res[m]],
                         lambda e, m=m, b=b: e.activation(out=o_fn(m), in_=banks[b][:, 0:T], func=AF.Copy))
                    stats_acc(7, banks[b][:, 0:T], [R_bank[b]], m == 0, m == KC - 1, defer=True)
                else:
                    S.op("act", [R_bank[b], R_cs], [o_res[m]],
                         lambda e, m=m, b=b: e.activation(out=o_fn(m), in_=banks[b][:, 0:T], func=AF.Identity,
                                                          bias=C(bias_name, m)))
                    stats_acc(7, banks[b][:, 0:T], [R_bank[b]], m == 0, m == KC - 1, bias=C(bias_name, m), defer=True)

        ext_rr = {"n": 0}

        def conv3(src_bank, wname, widx, car, car_res, mask_halo):
            i = ext_rr["n"] % 4
            ext_rr["n"] += 1
            S.group("act", [car_res, R_bank[src_bank]], [R_ext[i]], [
                lambda e: e.activation(out=ext[i][:, 0:2], in_=car, func=AF.Copy),
                lambda e: e.activation(out=ext[i][:, 2:2 + T], in_=banks[src_bank][:, 0:T], func=AF.Copy)])
            if mask_halo:
                S.op("dve", [R_ext[i], R_cs], [R_ext[i]],
                     lambda e: e.tensor_scalar(out=ext[i][:, 2:2 + HALO], in0=ext[i][:, 2:2 + HALO],
                                               scalar1=C("mask"), scalar2=None, op0=ALU.mult))
            S.group("act", [R_ext[i], R_cs], [car_res, R_acc[i]], [
                lambda e: e.activation(out=car, in_=ext[i][:, T:T + 2], func=AF.Copy),
                lambda e: e.activation(out=acc[i][:], in_=ext[i][:, 0:T], func=AF.Copy,
                                       scale=C(wname, widx * 3 + 0))])
            for k in (1, 2):
                S.op("dve", [R_ext[i], R_acc[i], R_cs], [R_acc[i]],
                     lambda e, k=k: e.scalar_tensor_tensor(out=acc[i][:], in0=ext[i][:, k:k + T],
                                                           scalar=C(wname, widx * 3 + k), in1=acc[i][:],
                                                           op0=ALU.mult, op1=ALU.add))
            return i

        def ffn(l, first_tile, last_layer):
            prenorm(f"g_ffn_pre{l}")
            REG_hid.use("hid")
            wname = f"f_w_up{l}"
            cname = f"f_conv{l}"
            mask = first_tile and l == 1
            for f in range(FC):
                sg = load_slab(wname, 0, KC, f * 128)
                sv = load_slab(wname, 0, KC, DFF + f * 128)
                bg, bv = next_bank(), next_bank()
                mm_group(bg, [(sg, KC)], xn, R_xn)
                mm_group(bv, [(sv, KC)], xn, R_xn)
                ig = conv3(bg, cname, f, carF[l][:, f, :], R_carF[l][f], mask)
                iv = conv3(bv, cname, FC + f, carF[l][:, FC + f, :], R_carF[l][FC + f], mask)
                S.op("act", [R_acc[ig]], [R_tmp[ig]],
                     lambda e: e.activation(out=tmp[ig][:], in_=acc[ig][:], func=AF.Silu))
                S.op("dve", [R_tmp[ig], R_acc[iv]], [R_hid[f]],
                     lambda e, f=f: e.tensor_tensor(out=hid(f), in0=tmp[ig][:], in1=acc[iv][:], op=ALU.mult))
            REG_bx.use("v1")
            out_proj(f"f_w_down{l}", FC, hid, R_hid, v1, R_v1)
            postnorm(f"g_ffn_post{l}", v1, R_v1, next_stats=not last_layer)

        def mixer_a():
            prenorm("g_mix_pre0")
            REG_hid.use("hid")
            for m in range(KC):
                s_b = load_slab("a_w_in", 0, KC, m * 128)
                s_c = load_slab("a_w_in", 0, KC, D + m * 128)
                s_h = load_slab("a_w_in", 0, KC, 2 * D + m * 128)
                bb, bcn, bh = next_bank(), next_bank(), next_bank()
                mm_group(bb, [(s_b, KC)], xn, R_xn)
                mm_group(bcn, [(s_c, KC)], xn, R_xn)
                mm_group(bh, [(s_h, KC)], xn, R_xn)
                i = ext_rr["n"] % 4
                ext_rr["n"] += 1
                S.op("act", [R_bank[bcn]], [R_tmp[i]],
                     lambda e: e.activation(out=tmp[i][:], in_=banks[bcn][:, 0:T], func=AF.Copy))
                S.op("act", [R_carA[m]], [R_ext[i]],
                     lambda e: e.activation(out=ext[i][:, 0:2], in_=carA[:, m, :], func=AF.Copy))
                S.op("dve", [R_bank[bh], R_tmp[i]], [R_ext[i]],
                     lambda e: e.tensor_tensor(out=ext[i][:, 2:2 + T], in0=banks[bh][:, 0:T], in1=tmp[i][:],
                                               op=ALU.mult))
                S.group("act", [R_ext[i], R_cs], [R_carA[m], R_acc[i]], [
                    lambda e: e.activation(out=carA[:, m, :], in_=ext[i][:, T:T + 2], func=AF.Copy),
                    lambda e: e.activation(out=acc[i][:], in_=ext[i][:, 0:T], func=AF.Copy,
                                           scale=C("a_conv", m * 3 + 0))])
                for k in (1, 2):
                    S.op("dve", [R_ext[i], R_acc[i], R_cs], [R_acc[i]],
                         lambda e, k=k: e.scalar_tensor_tensor(out=acc[i][:], in0=ext[i][:, k:k + T],
                                                               scalar=C("a_conv", m * 3 + k), in1=acc[i][:],
                                                               op0=ALU.mult, op1=ALU.add))
                S.op("dve", [R_bank[bb], R_acc[i]], [R_hid[m]],
                     lambda e: e.tensor_tensor(out=hid(m), in0=banks[bb][:, 0:T], in1=acc[i][:], op=ALU.mult))
            REG_bx.use("v1")
            out_proj("a_w_out", KC, hid, R_hid[0:KC], v1, R_v1)
            postnorm("g_mix_post0", v1, R_v1)

        def mixer_b(first_tile, use_pool):
            prenorm("g_mix_pre1")
            pstate["delay"] = 4
            REG_hid.use("v2")
            NK = CK - 1
            for m in range(KC):
                s_v = load_slab("b_w_pw1", 0, KC, m * 128)
                s_g = load_slab("b_w_pw1", 0, KC, D + m * 128)
                bv, bg = next_bank(), next_bank()
                mm_group(bv, [(s_v, KC)], xn, R_xn)
                mm_group(bg, [(s_g, KC)], xn, R_xn)
                i = ext_rr["n"] % 4
                ext_rr["n"] += 1
                S.op("act", [R_bank[bg], R_cs], [R_tmp[i]],
                     lambda e: e.activation(out=tmp[i][:], in_=banks[bg][:, 0:T], func=AF.Sigmoid,
                                            bias=C("b_b_pw1", KC + m)))
                S.op("act", [R_carB[m]], [R_ext[i]],
                     lambda e: e.activation(out=ext[i][:, 0:NK], in_=carB[:, m, :], func=AF.Copy))
                S.op("dve", [R_bank[bv], R_tmp[i], R_cs], [R_ext[i]],
                     lambda e: e.scalar_tensor_tensor(out=ext[i][:, NK:NK + T], in0=banks[bv][:, 0:T],
                                                      scalar=C("b_b_pw1", m), in1=tmp[i][:],
                                                      op0=ALU.add, op1=ALU.mult))
                if first_tile:
                    S.op("dve", [R_ext[i], R_cs], [R_ext[i]],
                         lambda e: e.tensor_scalar(out=ext[i][:, NK:NK + HALO], in0=ext[i][:, NK:NK + HALO],
                                                   scalar1=C("mask"), scalar2=None, op0=ALU.mult))
                S.group("act", [R_ext[i], R_cs], [R_carB[m], R_acc[i]], [
                    lambda e: e.activation(out=carB[:, m, :], in_=ext[i][:, T:T + NK], func=AF.Copy),
                    lambda e: e.activation(out=acc[i][:], in_=ext[i][:, 0:T], func=AF.Identity,
                                           scale=C("b_dw_w", m * CK + 0), bias=C("b_dw_b", m))])
                NP = NPOOL if use_pool else 0
                if NP:
                    j2 = m % 2
                    S.op("act", [R_ext[i], R_cs], [R_acc2[j2]],
                         lambda e: e.activation(out=acc2[j2][:], in_=ext[i][:, 1:1 + T], func=AF.Copy,
                                                scale=C("b_dw_w", m * CK + 1)))
                    for k in range(2, NP + 1):
                        jp = k % 2
                        S.op("act", [R_ext[i], R_cs], [R_ptmp[jp]],
                             lambda e, k=k, jp=jp: e.activation(out=ptmp[jp][:], in_=ext[i][:, k:k + T], func=AF.Copy,
                                                                scale=C("b_dw_w", m * CK + k)))
                        S.op("pool", [R_ptmp[jp], R_acc2[j2]], [R_acc2[j2]],
                             lambda e, jp=jp: e.tensor_tensor(out=acc2[j2][:], in0=acc2[j2][:], in1=ptmp[jp][:],
                                                              op=ALU.add))
                for k in range(NP + 1, CK):
                    last = (k == CK - 1) and not NP
                    S.op("dve", [R_ext[i], R_acc[i], R_cs], [R_v2[m]] if last else [R_acc[i]],
                         lambda e, k=k, last=last: e.scalar_tensor_tensor(
                             out=(v2(m) if last else acc[i][:]), in0=ext[i][:, k:k + T],
                             scalar=C("b_dw_w", m * CK + k), in1=acc[i][:], op0=ALU.mult, op1=ALU.add))
                if NP:
                    S.op("dve", [R_acc[i], R_acc2[j2]], [R_v2[m]],
                         lambda e: e.tensor_tensor(out=v2(m), in0=acc[i][:], in1=acc2[j2][:], op=ALU.add))
                stats_acc(6, v2(m), [R_v2[m]], m == 0, m == KC - 1, via_act=False, defer=True)
                stats_acc(7, v2(m), [R_v2[m]], m == 0, m == KC - 1, defer=True)
            pend_flush()
            pstate["delay"] = 1
            S.op("act", [R_bank[6]], [R_bc[0]],
                 lambda e: e.activation(out=bc[0][:], in_=banks[6][:, 0:T], func=AF.Copy, scale=1.0 / D))
            S.op("dve", [R_bc[0]], [R_bc[2]],
                 lambda e: e.tensor_tensor(out=bc[2][:], in0=bc[0][:], in1=bc[0][:], op=ALU.mult))
            S.op("dve", [R_bank[7], R_bc[2]], [R_bc[2]],
                 lambda e: e.scalar_tensor_tensor(out=bc[2][:], in0=banks[7][:, 0:T], scalar=1.0 / D, in1=bc[2][:],
                                                  op0=ALU.mult, op1=ALU.subtract))
            S.op("act", [R_bc[2], R_eps], [R_bc[2]],
                 lambda e: e.activation(out=bc[2][:], in_=bc[2][:], func=AF.Sqrt, bias=eps_t[:, 1:2], scale=1.0))
            S.op("dve", [R_bc[2]], [R_bc[2]], lambda e: e.reciprocal(bc[2][:], bc[2][:]))
            for m in range(KC):
                i = m % 4
                S.op("dve", [R_v2[m], R_bc[0]], [R_tmp[i]],
                     lambda e: e.tensor_tensor(out=tmp[i][:], in0=v2(m), in1=bc[0][:], op=ALU.subtract))
                S.op("dve", [R_tmp[i], R_bc[2]], [R_tmp[i]],
                     lambda e: e.tensor_tensor(out=tmp[i][:], in0=tmp[i][:], in1=bc[2][:], op=ALU.mult))
                S.op("act", [R_tmp[i], R_cs], [R_xn[m]],
                     lambda e: e.activation(out=xn(m), in_=tmp[i][:], func=AF.Silu,
                                            scale=C("b_ln_g", m), bias=C("b_ln_b", m)))
            out_proj("b_w_pw2", KC, xn, R_xn, v2, R_v2, bias_name="b_b_pw2")
            postnorm("g_mix_post1", v2, R_v2)

        XQ = "act"
        if LM:
            xs_fm = nc.dram_tensor("xs_fm", [NRUN, 128, KC * T], F32, kind="Internal").ap()
            R_xsf = [Res(f"xsf{i}") for i in range(NRUN)]
            sem_xsp = S.newsem("d_xsp")

        def load_x_tokmajor(it):
            t0 = it * T
            REG_hid.use("stg")
            for b in range(NB):
                n = min(128, T - b * 128)
                S.dma(XQ, sem_stg[b], stg(b)[0:n, :], x_d[t0 + b * 128:t0 + b * 128 + n, :], [], [R_stg[b]])
            for c in range(KC):
                bk = next_bank()
                fns = []
                for b in range(NB):
                    n = min(128, T - b * 128)
                    fns.append(lambda e, b=b, n=n: e.transpose(
                        out=banks[bk][:, b * 128:b * 128 + n], in_=stg(b)[0:n, c * 128:(c + 1) * 128],
                        identity=ident[0:n, 0:n]))
                S.group("pe", R_stg + [R_const], [R_bank[bk]], fns)
                if c % 2 == 0:
                    S.op("act", [R_bank[bk]], [R_x[c]],
                         lambda e: e.activation(out=xres[:, c, :], in_=banks[bk][:, 0:T], func=AF.Copy))
                else:
                    S.op("dve", [R_bank[bk]], [R_x[c]],
                         lambda e: e.tensor_copy(out=xres[:, c, :], in_=banks[bk][:, 0:T]))
                stats_acc(6, xres[:, c, :], [R_x[c]], c == 0, c == KC - 1)

        def store_y(it):
            t0 = it * T
            REG_hid.use("stg")
            for b in range(NB):
                n = min(128, T - b * 128)
                lo = max(0, HALO - (t0 + b * 128))
                if lo >= n:
                    continue
                for c4 in range(KC // 4):
                    bk = next_bank()
                    fns = []
                    for j in range(4):
                        c = c4 * 4 + j
                        fns.append(lambda e, c=c, j=j: e.transpose(
                            out=banks[bk][0:n, j * 128:(j + 1) * 128], in_=xres[:, c, b * 128:b * 128 + n],
                            identity=ident))
                    S.group("pe", R_x[c4 * 4:c4 * 4 + 4] + [R_const], [R_bank[bk]], fns)
                    if c4 % 2 == 0:
                        S.op("act", [R_bank[bk]], [R_stg[b]],
                             lambda e: e.activation(out=stg(b)[0:n, c4 * 512:(c4 + 1) * 512], in_=banks[bk][0:n, :],
                                                    func=AF.Copy))
                    else:
                        S.op("dve", [R_bank[bk]], [R_stg[b]],
                             lambda e: e.tensor_copy(out=stg(b)[0:n, c4 * 512:(c4 + 1) * 512], in_=banks[bk][0:n, :]))
                g0 = t0 + b * 128 + lo - HALO
                S.dma(XQ, sem_stg[b], y_d[g0:g0 + (n - lo), :], stg(b)[lo:n, :], [R_stg[b]], [])

        if not LM:
            for it in range(NRUN):
                wstate["tile"] = it
                wstate["s"] = 0
                load_x_tokmajor(it)
                for l in layers:
                    if l == 0:
                        mixer_a()
                    else:
                        mixer_b(it == 0, False)
                    ffn(l, it == 0, l == layers[-1])
                store_y(it)
        else:
            for it in range(NRUN):
                wstate["tile"] = it
                wstate["s"] = 0
                bgstate["on"] = it >= 1
                load_x_tokmajor(it)
                mixer_a()
                ffn(0, it == 0, True)
                S.dma(XQ, sem_xsp, xs_fm[it], xres[:].rearrange("p k t -> p (k t)"), R_x, [R_xsf[it]])
            bgstate["on"] = False
            while bgstate["next"] < len(bg_list):
                bg_convert_one()
            for it in range(NRUN):
                wstate["tile"] = it
                wstate["s"] = NS0
                S.dma(XQ, sem_xsp, xres[:].rearrange("p k t -> p (k t)"), xs_fm[it], [R_xsf[it]], R_x)
                for c in range(KC):
                    stats_acc(6, xres[:, c, :], [R_x[c]], c == 0, c == KC - 1)
                mixer_b(it == 0, False)
                ffn(1, it == 0, True)
                store_y(it)
        S.final_wait(XQ, R_stg)
    return nc


def _fm(v):
    v = np.asarray(v, np.float32)
    return np.ascontiguousarray(v.reshape(-1, 128).T)


def pack_consts(inp, mask_val):
    cs = np.zeros((128, NCONST), np.float32)

    def put(name, arr):
        arr = np.asarray(arr, np.float32)
        cs[:, OFF[name]:OFF[name] + arr.shape[1]] = arr

    for l in range(2):
        put(f"g_mix_pre{l}", _fm(inp["g_mix_pre"][l]))
        put(f"g_mix_post{l}", _fm(inp["g_mix_post"][l]))
        put(f"g_ffn_pre{l}", _fm(inp["g_ffn_pre"][l]))
        put(f"g_ffn_post{l}", _fm(inp["g_ffn_post"][l]))
        fc = np.asarray(inp["f_conv_w"][l], np.float32)
        fc = fc.T.reshape(2 * FC, 128, 3).transpose(1, 0, 2).reshape(128, 2 * FC * 3)
        put(f"f_conv{l}", fc)
    ac = np.asarray(inp["a_conv_w"][0], np.float32).T.reshape(KC, 128, 3).transpose(1, 0, 2).reshape(128, KC * 3)
    put("a_conv", ac)
    put("b_b_pw1", _fm(inp["b_b_pw1"][0]))
    dw = np.asarray(inp["b_dw_w"][0], np.float32).T.reshape(KC, 128, CK).transpose(1, 0, 2).reshape(128, KC * CK)
    put("b_dw_w", dw)
    put("b_dw_b", _fm(inp["b_dw_b"][0]))
    put("b_ln_g", _fm(inp["b_ln_g"][0]))
    put("b_ln_b", _fm(inp["b_ln_b"][0]))
    put("b_b_pw2", _fm(inp["b_b_pw2"][0]))
    cs[:, OFF["mask"]] = mask_val
    cs[:, OFF["ident"]:OFF["ident"] + 128] = np.eye(128, dtype=np.float32)
    cs[:, OFF["ones"]:OFF["ones"] + 128] = 1.0
    return cs


T_TILE = 298
N_TILE = 7


def make_in_maps(inp, cores, T=T_TILE, NTILE=N_TILE):
    NT = T * NTILE
    HALO = NT - CHUNK
    x = np.asarray(inp["x"], np.float32)
    wmap = {
        "a_w_in": np.asarray(inp["a_w_in"][0], np.float32), "a_w_out": np.asarray(inp["a_w_out"][0], np.float32),
        "b_w_pw1": np.asarray(inp["b_w_pw1"][0], np.float32), "b_w_pw2": np.asarray(inp["b_w_pw2"][0], np.float32),
        "f_w_up0": np.asarray(inp["f_w_up"][0], np.float32), "f_w_up1": np.asarray(inp["f_w_up"][1], np.float32),
        "f_w_down0": np.asarray(inp["f_w_down"][0], np.float32),
        "f_w_down1": np.asarray(inp["f_w_down"][1], np.float32),
    }
    maps = []
    for c in cores:
        b, s = divmod(c, NCORE // BATCH)
        start = s * CHUNK
        xs = np.zeros((NT, D), np.float32)
        lo = start - HALO
        if lo >= 0:
            xs[:] = x[b, lo:start + CHUNK]
        else:
            xs[-lo:] = x[b, 0:start + CHUNK]
        m = dict(wmap)
        m["xs"] = xs
        m["consts"] = pack_consts(inp, 0.0 if s == 0 else 1.0)
        maps.append(m)
    return maps


def kernel(**inputs):
    T, NTILE = T_TILE, N_TILE
    HALO = T * NTILE - CHUNK
    nc = build_program(T, NTILE, NTILE, HALO)
    cores = list(range(NCORE))
    in_maps = make_in_maps(inputs, cores, T, NTILE)
    res = run_bass_kernel_spmd(nc, in_maps, core_ids=cores)
    out = np.empty((BATCH, SEQ, D), np.float32)
    for c in cores:
        b, s = divmod(c, NCORE // BATCH)
        out[b, s * CHUNK:(s + 1) * CHUNK] = res.results[c]["y"]
    return out
```
